# Optimizing a Trainium2 kernel written in Bass

```python
import math
import jax
import jax.numpy as jnp
from jax import lax
import numpy as np

D_MODEL = 1024
BATCH = 8
SEQ = 8192
DEPTH = 4

N_HEADS = 8
N_KV_HEADS = 2
HEAD_DIM = D_MODEL // N_HEADS
GROUP = N_HEADS // N_KV_HEADS
Q_DIM = N_HEADS * HEAD_DIM
KV_DIM = N_KV_HEADS * HEAD_DIM
WINDOW = 128
BLOCK = 128
NUM_BUCKETS = 32
MAX_DISTANCE = 128
LRU_WIDTH = D_MODEL
LRU_BLOCKS = 8
LRU_BLOCK_DIM = LRU_WIDTH // LRU_BLOCKS
LRU_C = 8.0
CONV_WIDTH = 4
CONV_LEFT = 2
N_DIRS = 2
D_FF = 4 * D_MODEL
EPS = 1e-6
SPLITS = (Q_DIM, KV_DIM, KV_DIM, LRU_WIDTH, LRU_WIDTH, D_MODEL, D_MODEL)
IN_COLS = sum(SPLITS)

kernel_name = "hybrid_swa_rglru_encoder"


def rms_norm(x, g):
    xf = x.astype(jnp.float32)
    y = xf * lax.rsqrt(jnp.mean(xf * xf, axis=-1, keepdims=True) + EPS)
    return (y * g.astype(jnp.float32)).astype(x.dtype)


def t5_bucket(rel):
    half = NUM_BUCKETS // 2
    max_exact = half // 2
    n = jnp.abs(rel)
    large = max_exact + (jnp.log(jnp.maximum(n, 1).astype(jnp.float32) / max_exact)
                         / math.log(MAX_DISTANCE / max_exact) * (half - max_exact)).astype(jnp.int32)
    large = jnp.minimum(large, half - 1)
    return jnp.where(rel > 0, half, 0) + jnp.where(n < max_exact, n, large)


def band_blocks(t):
    b, s = t.shape[0], t.shape[1]
    nb = s // BLOCK
    tp = jnp.pad(t, ((0, 0), (BLOCK, BLOCK), (0, 0), (0, 0)))
    tb = tp.reshape(b, nb + 2, BLOCK, t.shape[2], t.shape[3])
    return jnp.concatenate([tb[:, :-2], tb[:, 1:-1], tb[:, 2:]], axis=2)


def windowed_gqa(q, k, v, sink, rel_bias):
    b, s = q.shape[0], q.shape[1]
    nb = s // BLOCK
    qb = q.reshape(b, nb, BLOCK, N_KV_HEADS, GROUP, HEAD_DIM)
    kb = band_blocks(k.reshape(b, s, N_KV_HEADS, HEAD_DIM))
    vb = band_blocks(v.reshape(b, s, N_KV_HEADS, HEAD_DIM))
    scores = jnp.einsum('bnqkgd,bnjkd->bnkgqj', qb, kb).astype(jnp.float32) * (HEAD_DIM ** -0.5)
    q_off = jnp.arange(BLOCK, dtype=jnp.int32)[:, None]
    j_off = jnp.arange(3 * BLOCK, dtype=jnp.int32)[None, :] - BLOCK
    rel = j_off - q_off
    bias = rel_bias.astype(jnp.float32)[t5_bucket(rel)]
    bias = jnp.transpose(bias, (2, 0, 1)).reshape(N_KV_HEADS, GROUP, BLOCK, 3 * BLOCK)
    k_abs = jnp.arange(nb, dtype=jnp.int32)[:, None] * BLOCK + j_off
    valid = (jnp.abs(rel) <= WINDOW)[None] & ((k_abs >= 0) & (k_abs < s))[:, None, :]
    logits = jnp.where(valid[None, :, None, None], scores + bias, -jnp.inf)
    sink_l = sink.astype(jnp.float32).reshape(1, 1, N_KV_HEADS, GROUP, 1, 1)
    m = jnp.maximum(jnp.max(logits, axis=-1, keepdims=True), sink_l)
    p = jnp.exp(logits - m)
    denom = jnp.sum(p, axis=-1, keepdims=True) + jnp.exp(sink_l - m)
    p = (p / denom).astype(v.dtype)
    out = jnp.einsum('bnkgqj,bnjkd->bnqkgd', p, vb)
    return out.reshape(b, s, Q_DIM)


def _lin_combine(e1, e2):
    a1, b1 = e1
    a2, b2 = e2
    return a1 * a2, a2 * b1 + b2


def rg_lru_bidir(xr, gate, conv_w, conv_b, w_rg, b_rg, w_ig, b_ig, lam):
    b, s = xr.shape[0], xr.shape[1]
    xf = xr.astype(jnp.float32)
    xp = jnp.pad(xf, ((0, 0), (CONV_LEFT, CONV_WIDTH - 1 - CONV_LEFT), (0, 0)))
    cw = conv_w.astype(jnp.float32)
    xc = conv_b.astype(jnp.float32) + xp[:, 0:s] * cw[0]
    for j in range(1, CONV_WIDTH):
        xc = xc + xp[:, j:j + s] * cw[j]
    xblk = xc.reshape(b, s, LRU_BLOCKS, LRU_BLOCK_DIM)
    outs = []
    for d in range(N_DIRS):
        r = jax.nn.sigmoid(jnp.einsum('bsnh,nhk->bsnk', xblk, w_rg[d].astype(jnp.float32)).reshape(b, s, LRU_WIDTH)
                           + b_rg[d].astype(jnp.float32))
        i = jax.nn.sigmoid(jnp.einsum('bsnh,nhk->bsnk', xblk, w_ig[d].astype(jnp.float32)).reshape(b, s, LRU_WIDTH)
                           + b_ig[d].astype(jnp.float32))
        log_a = -LRU_C * r * jax.nn.softplus(-lam[d].astype(jnp.float32))
        a = jnp.exp(log_a)
        mult = jnp.sqrt(-jnp.expm1(2.0 * log_a))
        _, hs = lax.associative_scan(_lin_combine, (a, mult * (i * xc)), axis=1, reverse=(d == 1))
        outs.append(hs)
    y = outs[0] + outs[1]
    return (y * jax.nn.gelu(gate.astype(jnp.float32))).astype(xr.dtype)


def setup_inputs(seed: int = 0) -> dict:
    key = jax.random.key(seed)
    ks = jax.random.split(key, 20)
    f32 = jnp.float32

    def nrm(k, shape, scale):
        return jax.random.normal(k, shape, f32) * scale

    x = nrm(ks[0], (BATCH, SEQ, D_MODEL), 1.0)
    w_in = nrm(ks[1], (DEPTH, D_MODEL, IN_COLS), D_MODEL ** -0.5)
    w_o_attn = nrm(ks[2], (DEPTH, Q_DIM, D_MODEL), Q_DIM ** -0.5)
    w_o_lru = nrm(ks[3], (DEPTH, LRU_WIDTH, D_MODEL), LRU_WIDTH ** -0.5)
    w_out = nrm(ks[4], (DEPTH, D_MODEL, D_MODEL), D_MODEL ** -0.5)
    attn_sink = nrm(ks[5], (DEPTH, N_HEADS), 0.5)
    rel_bias = nrm(ks[6], (NUM_BUCKETS, N_HEADS), 0.5)
    conv_w = nrm(ks[7], (DEPTH, CONV_WIDTH, LRU_WIDTH), CONV_WIDTH ** -0.5)
    conv_b = nrm(ks[8], (DEPTH, LRU_WIDTH), 0.02)
    w_rgate = nrm(ks[9], (DEPTH, N_DIRS, LRU_BLOCKS, LRU_BLOCK_DIM, LRU_BLOCK_DIM), LRU_BLOCK_DIM ** -0.5)
    b_rgate = nrm(ks[10], (DEPTH, N_DIRS, LRU_WIDTH), 0.02)
    w_igate = nrm(ks[11], (DEPTH, N_DIRS, LRU_BLOCKS, LRU_BLOCK_DIM, LRU_BLOCK_DIM), LRU_BLOCK_DIM ** -0.5)
    b_igate = nrm(ks[12], (DEPTH, N_DIRS, LRU_WIDTH), 0.02)
    a_c = jax.random.uniform(ks[13], (DEPTH, N_DIRS, LRU_WIDTH), f32, 0.9, 0.999)
    a = a_c ** (1.0 / LRU_C)
    lru_lambda = jnp.log(a) - jnp.log1p(-a)
    norm1_g = 1.0 + nrm(ks[14], (DEPTH, D_MODEL), 0.02)
    norm2_g = 1.0 + nrm(ks[15], (DEPTH, D_MODEL), 0.02)
    w_mlp_up = nrm(ks[16], (DEPTH, D_MODEL, D_FF), D_MODEL ** -0.5)
    w_mlp_down = nrm(ks[17], (DEPTH, D_FF, D_MODEL), 0.5 * D_FF ** -0.5)
    final_norm_g = 1.0 + nrm(ks[18], (D_MODEL,), 0.02)
    return {"x": x, "w_in": w_in, "w_o_attn": w_o_attn, "w_o_lru": w_o_lru, "w_out": w_out,
            "attn_sink": attn_sink, "rel_bias": rel_bias, "conv_w": conv_w, "conv_b": conv_b,
            "w_rgate": w_rgate, "b_rgate": b_rgate, "w_igate": w_igate, "b_igate": b_igate,
            "lru_lambda": lru_lambda, "norm1_g": norm1_g, "norm2_g": norm2_g,
            "w_mlp_up": w_mlp_up, "w_mlp_down": w_mlp_down, "final_norm_g": final_norm_g}


def reference(x, w_in, w_o_attn, w_o_lru, w_out, attn_sink, rel_bias, conv_w, conv_b,
              w_rgate, b_rgate, w_igate, b_igate, lru_lambda, norm1_g, norm2_g,
              w_mlp_up, w_mlp_down, final_norm_g):
    cuts = np.cumsum(SPLITS)[:-1].tolist()
    for l in range(DEPTH):
        h = rms_norm(x, norm1_g[l])
        q, k, v, xr, gr, ga, gb = jnp.split(h @ w_in[l], cuts, axis=-1)
        ya = windowed_gqa(q, k, v, attn_sink[l], rel_bias) @ w_o_attn[l]
        yb = rg_lru_bidir(xr, gr, conv_w[l], conv_b[l], w_rgate[l], b_rgate[l],
                          w_igate[l], b_igate[l], lru_lambda[l]) @ w_o_lru[l]
        mix = jax.nn.sigmoid(ga) * ya + jax.nn.sigmoid(gb) * yb
        x = x + mix @ w_out[l]
        h = rms_norm(x, norm2_g[l])
        x = x + jnp.square(jax.nn.relu(h @ w_mlp_up[l])) @ w_mlp_down[l]
    return rms_norm(x, final_norm_g)
```

```python
import math
from contextlib import ExitStack

import numpy as np
import concourse.bass as bass
import concourse.mybir as mybir
from concourse.bass_utils import run_bass_kernel_spmd

F32 = mybir.dt.float32
BF16 = mybir.dt.bfloat16
AF = mybir.ActivationFunctionType
ALU = mybir.AluOpType

D = 1024
FT = 8
DEPTH = 4
SEQ = 8192
NCORES = 8
CH = 512
IN_COLS = 5632
DFF = 4096
EPS = 1e-6
MASK_NEG = -30000.0
OQ, OK_, OV, OXR, OGR, OGA, OGB = 0, 1024, 1280, 1536, 2560, 3584, 4608
PL = 104
C_N1, C_N2, C_CW, C_CB, C_BR, C_BI, C_LAM = 0, 8, 16, 48, 56, 72, 88
GELU_C = math.sqrt(2.0 / math.pi)


class Sem:
    def __init__(self, h, name):
        self.h = h
        self.name = name
        self.v = 0


class Buf:
    def __init__(self, name, t, dram=False):
        self.name = name
        self.t = t
        self.dram = dram
        self.writers = {}
        self.readers = {}
        self.psum = False

    def __getitem__(self, k):
        return self.t[k]


class Eng:
    def __init__(self, name, e, sem):
        self.name = name
        self.e = e
        self.sem = sem
        self.cnt = 0
        self.seen = {}

    def wait_tok(self, sem, val):
        if sem is self.sem and val > self.cnt:
            return
        if self.seen.get(sem, 0) >= val:
            return
        self.e.wait_ge(sem.h, val)
        self.seen[sem] = val


class K:
    def __init__(self, seq=SEQ, depth=DEPTH, debug=False):
        self.seq = seq
        self.depth = depth
        self.nch = seq // CH
        self.debug = debug
        self.nc = bass.Bass("TRN2", target_bir_lowering=False)
        self.es = ExitStack()
        self.sems = []
        self.dma_sems = {}
        self.n_ins = 0
        self.n_alloc = 0

    def new_sem(self, name):
        h = self.es.enter_context(self.nc.semaphore(name))
        s = Sem(h, name)
        self.sems.append(s)
        return s

    def dram(self, name, shape, dt, kind="Internal"):
        t = self.nc.dram_tensor(name, list(shape), dt, kind=kind)
        return Buf(name, t.ap(), dram=True)

    def sb(self, stack, name, shape, dt):
        self.n_alloc += 1
        t = stack.enter_context(self.nc.sbuf_tensor("%s_%d" % (name, self.n_alloc), list(shape), dt))
        return Buf(name, t)

    def op(self, eng, emit, reads=(), writes=(), signal=True):
        for b in reads:
            for s, v in b.writers.items():
                eng.wait_tok(s, v)
            if b.psum:
                for s, v in b.readers.items():
                    eng.wait_tok(s, v)
        for b in writes:
            for s, v in b.writers.items():
                eng.wait_tok(s, v)
            for s, v in b.readers.items():
                eng.wait_tok(s, v)
        ins = emit()
        self.n_ins += 1
        val = eng.cnt + 1
        if signal:
            ins.then_inc(eng.sem.h, 1)
            eng.cnt = val
        for b in reads:
            if b.readers.get(eng.sem, 0) < val:
                b.readers[eng.sem] = val
        for b in writes:
            b.writers = {eng.sem: val}
            b.readers = {}
        return ins

    def dma(self, q, dst, dst_ap, src, src_ap, accumulate=False, **kw):
        key = dst.name if not dst.dram else (src.name if not src.dram else dst.name)
        sem = self.dma_sems.get(key)
        if sem is None:
            sem = self.new_sem("d%d" % len(self.dma_sems))
            self.dma_sems[key] = sem
        for s, v in src.writers.items():
            q.wait_tok(s, v)
        if not (dst.dram or accumulate):
            for s, v in dst.writers.items():
                q.wait_tok(s, v)
        for s, v in dst.readers.items():
            q.wait_tok(s, v)
        ins = q.e.dma_start(out=dst_ap, in_=src_ap, **kw)
        self.n_ins += 1
        sem.v += 16
        ins.then_inc(sem.h, 16)
        src.readers[sem] = sem.v
        if dst.dram or accumulate:
            dst.writers[sem] = sem.v
            if not dst.dram:
                dst.readers = {}
        else:
            dst.writers = {sem: sem.v}
            dst.readers = {}

    def barrier(self):
        engs = [self.PE, self.ACT, self.DVE, self.POOL, self.SP]
        for e in engs:
            for s in self.sems:
                cur = s.v
                for e2 in engs:
                    if e2.sem is s:
                        cur = e2.cnt
                if cur > 0:
                    e.wait_tok(s, cur)

    def mm(self, ps_buf, out_ap, lhsT_buf, lhsT_ap, rhs_buf, rhs_ap, start, stop, signal=None):
        if signal is None:
            signal = stop
        return self.op(self.PE, lambda: self.nc.tensor.matmul(out_ap, lhsT=lhsT_ap, rhs=rhs_ap, start=start, stop=stop),
                       reads=(lhsT_buf, rhs_buf), writes=(ps_buf,), signal=signal)

    def act(self, out_buf, out_ap, in_buf, in_ap, func, bias=None, scale=None, extra_reads=()):
        kw = {}
        if bias is not None:
            kw["bias"] = bias
        if scale is not None:
            kw["scale"] = scale
        return self.op(self.ACT, lambda: self.nc.scalar.activation(out=out_ap, in_=in_ap, func=func, **kw),
                       reads=(in_buf,) + tuple(extra_reads), writes=(out_buf,))

    def tt(self, eng, out_buf, out_ap, a_buf, a_ap, b_buf, b_ap, op):
        e = eng.e
        return self.op(eng, lambda: e.tensor_tensor(out=out_ap, in0=a_ap, in1=b_ap, op=op),
                       reads=(a_buf, b_buf), writes=(out_buf,))

    def stt(self, out_buf, out_ap, a_buf, a_ap, scalar, b_buf, b_ap, op0, op1, extra_reads=()):
        return self.op(self.DVE, lambda: self.nc.vector.scalar_tensor_tensor(
            out=out_ap, in0=a_ap, scalar=scalar, in1=b_ap, op0=op0, op1=op1),
            reads=(a_buf, b_buf) + tuple(extra_reads), writes=(out_buf,))

    def ts(self, eng, out_buf, out_ap, a_buf, a_ap, s1, op0, s2=None, op1=None, extra_reads=()):
        e = eng.e
        if op1 is None:
            return self.op(eng, lambda: e.tensor_scalar(out=out_ap, in0=a_ap, scalar1=s1, scalar2=None, op0=op0),
                           reads=(a_buf,) + tuple(extra_reads), writes=(out_buf,))
        return self.op(eng, lambda: e.tensor_scalar(out=out_ap, in0=a_ap, scalar1=s1, scalar2=s2, op0=op0, op1=op1),
                       reads=(a_buf,) + tuple(extra_reads), writes=(out_buf,))

    def copy(self, eng, out_buf, out_ap, in_buf, in_ap):
        e = eng.e
        return self.op(eng, lambda: e.tensor_copy(out=out_ap, in_=in_ap), reads=(in_buf,), writes=(out_buf,))

    def memset(self, eng, buf, ap, val):
        e = eng.e
        return self.op(eng, lambda: e.memset(ap, val), reads=(), writes=(buf,))

    def next_ps(self):
        b = self.psum[self.ps_i % len(self.psum)]
        self.ps_i += 1
        return b

    def build(self):
        nc = self.nc
        S = self.seq
        L = self.depth
        es = self.es
        self.PE = Eng("pe", nc.tensor, self.new_sem("s_pe"))
        self.ACT = Eng("act", nc.scalar, self.new_sem("s_act"))
        self.DVE = Eng("dve", nc.vector, self.new_sem("s_dve"))
        self.POOL = Eng("pool", nc.gpsimd, self.new_sem("s_pool"))
        self.SP = Eng("sp", nc.sync, self.new_sem("s_sp"))

        ext = "ExternalInput"
        self.xT = self.dram("xT", [D, S], F32, ext)
        self.w_in = self.dram("w_in", [L, D, IN_COLS], F32, ext)
        self.w_oa = self.dram("w_o_attn", [L, D, D], F32, ext)
        self.w_ol = self.dram("w_o_lru", [L, D, D], F32, ext)
        self.w_out = self.dram("w_out", [L, D, D], F32, ext)
        self.w_up = self.dram("w_mlp_up", [L, D, DFF], F32, ext)
        self.w_dn = self.dram("w_mlp_down", [L, DFF, D], F32, ext)
        self.w_rg = self.dram("w_rgate", [L, 2, 8, 128, 128], F32, ext)
        self.w_ig = self.dram("w_igate", [L, 2, 8, 128, 128], F32, ext)
        self.spk = self.dram("spk", [128, L * PL + 8], F32, ext)
        self.sinkrep = self.dram("sinkrep", [128, L, 1024], F32, ext)
        self.biasg = self.dram("biasg", [128, 3, 8, 128], F32, ext)
        self.maskc = self.dram("maskc", [128, 3, 8, 128], F32, ext)
        self.identc = self.dram("identc", [128, 128], F32, ext)
        self.outT = self.dram("outT", [D, S], F32, "ExternalOutput")
        dk = "ExternalOutput" if self.debug else "Internal"
        self.b_win = [self.dram("b_win%d" % l, [D, IN_COLS], BF16) for l in range(L)]
        self.b_woa = [self.dram("b_woa%d" % l, [D, D], BF16) for l in range(L)]
        self.b_wol = [self.dram("b_wol%d" % l, [D, D], BF16) for l in range(L)]
        self.b_wout = [self.dram("b_wout%d" % l, [D, D], BF16) for l in range(L)]
        self.b_wup = [self.dram("b_wup%d" % l, [D, DFF], BF16) for l in range(L)]
        self.b_wdn = [self.dram("b_wdn%d" % l, [DFF, D], BF16) for l in range(L)]
        self.b_wrg = [self.dram("b_wrg%d" % l, [2, 8, 128, 128], BF16) for l in range(L)]
        self.b_wig = [self.dram("b_wig%d" % l, [2, 8, 128, 128], BF16) for l in range(L)]
        self.d_q = self.dram("d_q", [D, S], BF16, dk)
        self.d_k = self.dram("d_k", [256, S], BF16, dk)
        self.d_v = self.dram("d_v", [S, 256], BF16, dk)
        self.d_xr = self.dram("d_xr", [D, S], BF16, dk)
        self.d_gr = self.dram("d_gr", [D, S], BF16, dk)
        self.d_ta = self.dram("d_ta", [D, S], BF16, dk)
        self.d_tb = self.dram("d_tb", [D, S], BF16, dk)
        self.d_xc = self.dram("d_xc", [D, S], BF16, dk)
        self.d_hf = self.dram("d_hf", [D, S], BF16, dk)
        self.d_mb = self.dram("d_mb", [D, S], BF16, dk)
        self.d_x1 = self.dram("d_x1", [D, S], F32, dk)
        self.d_x2 = self.dram("d_x2", [D, S], F32, dk)
        self.d_bth = self.dram("d_bth", [128, 3, 8, 128], BF16)
        self.d_btl = self.dram("d_btl", [128, 3, 8, 128], BF16)
        if self.debug:
            self.d_hb = self.dram("d_hb", [D, S], F32, dk)
            self.d_yb = self.dram("d_yb", [D, S], BF16, dk)
            self.d_ao = self.dram("d_ao", [D, S], BF16, dk)

        self.psum = []
        for i in range(8):
            t = es.enter_context(nc.psum_tensor("ps%d" % i, [128, 512], F32))
            self.psum.append(Buf("ps%d" % i, t))
            self.psum[-1].psum = True
        self.ps_i = 0

        g = es
        self.SPK = self.sb(g, "SPK", [128, L * PL + 8], F32)
        self.DER = self.sb(g, "DER", [128, L * 64], F32)
        self.ONES = self.sb(g, "ONES", [128, 128], BF16)
        self.ONESM = self.sb(g, "ONESM", [128, 128], BF16)
        self.IDENT = self.sb(g, "IDENT", [128, 128], BF16)
        self.QTR = self.sb(g, "QTR", [128, 1], F32)
        self.IDF = self.sb(g, "IDF", [128, 128], F32)

        self.setup()
        for l in range(L):
            self.layer(l)
        self.barrier()
        es.close()
        return nc

    def cast_w(self, dst, src_ap, rows, cols):
        r = rows // 128
        s_ap = src_ap.rearrange("(p r) n -> p (r n)", p=128)
        d_ap = dst.t.rearrange("(p r) n -> p (r n)", p=128)
        tot = r * cols
        step = 8192
        srcbuf = self._cast_src
        for c0 in range(0, tot, step):
            c1 = min(tot, c0 + step)
            self.dma(self.POOL, dst, d_ap[:, c0:c1], srcbuf, s_ap[:, c0:c1], max_dma_last_dim=4096)

    def setup(self):
        nc = self.nc
        L = self.depth
        for l in range(L):
            for (dst, src, rows, cols) in ((() if l == 0 else ((self.b_win[l], self.w_in, D, IN_COLS),)) + (
                (self.b_wol[l], self.w_ol, D, D),
                (self.b_woa[l], self.w_oa, D, D),
                (self.b_wout[l], self.w_out, D, D),
                (self.b_wup[l], self.w_up, D, DFF),
                (self.b_wdn[l], self.w_dn, DFF, D),
            )):
                self._cast_src = src
                self.cast_w(dst, src.t[l], rows, cols)
            for (dst, src) in ((self.b_wrg[l], self.w_rg), (self.b_wig[l], self.w_ig)):
                s_ap = src.t[l].rearrange("d n h k -> (d n) (h k)")
                d_ap = dst.t.rearrange("d n h k -> (d n) (h k)")
                self.dma(self.POOL, dst, d_ap, src, s_ap, max_dma_last_dim=4096)
        self.dma(self.SP, self.SPK, self.SPK[:, :], self.spk, self.spk.t[:, :])
        with ExitStack() as st:
            TMP = self.sb(st, "su_tmp", [128, 3, 8, 128], F32)
            TMP2 = self.sb(st, "su_tmp2", [128, 3, 8, 128], F32)
            TMP3 = self.sb(st, "su_tmp3", [128, 3, 8, 128], F32)
            IDF = self.IDF
            self.BTH = self.sb(st, "su_BTH", [128, 3, 8, 128], BF16)
            self.BTL = self.sb(st, "su_BTL", [128, 3, 8, 128], BF16)
            E1 = self.sb(st, "su_e1", [128, L * 16], F32)
            self.memset(self.DVE, self.ONES, self.ONES[:, :], 1.0)
            self.memset(self.DVE, self.ONESM, self.ONESM[:, :], 1.0 / 1024.0)
            self.memset(self.DVE, self.QTR, self.QTR[:, :], 0.25000003)
            self.dma(self.SP, IDF, IDF[:, :], self.identc, self.identc.t[:, :])
            self.copy(self.DVE, self.IDENT, self.IDENT[:, :], IDF, IDF[:, :])
            self.dma(self.SP, TMP, TMP[:, :, :, :], self.biasg, self.biasg.t[:, :, :, :])
            self.dma(self.SP, TMP2, TMP2[:, :, :, :], self.maskc, self.maskc.t[:, :, :, :])
            self.tt(self.DVE, TMP, TMP[:, :, :, :], TMP, TMP[:, :, :, :], TMP2, TMP2[:, :, :, :], ALU.add)
            self.copy(self.DVE, self.BTH, self.BTH[:, :, :, :], TMP, TMP[:, :, :, :])
            self.copy(self.DVE, TMP2, TMP2[:, :, :, :], self.BTH, self.BTH[:, :, :, :])
            self.tt(self.DVE, TMP3, TMP3[:, :, :, :], TMP, TMP[:, :, :, :], TMP2, TMP2[:, :, :, :], ALU.subtract)
            self.copy(self.DVE, self.BTL, self.BTL[:, :, :, :], TMP3, TMP3[:, :, :, :])
            self.dma(self.SP, self.d_bth, self.d_bth.t[:, :, :, :], self.BTH, self.BTH[:, :, :, :])
            self.dma(self.SP, self.d_btl, self.d_btl.t[:, :, :, :], self.BTL, self.BTL[:, :, :, :])
            for l in range(L):
                b0 = l * PL
                d0 = l * 64
                self.ts(self.DVE, self.DER, self.DER[:, d0:d0 + 32], self.SPK, self.SPK[:, b0 + C_BR:b0 + C_BR + 32],
                        0.5, ALU.mult)
                self.act(E1, E1[:, l * 16:(l + 1) * 16], self.SPK, self.SPK[:, b0 + C_LAM:b0 + C_LAM + 16],
                         AF.Exp, scale=-1.0)
            for l in range(L):
                d0 = l * 64
                e_ap = E1[:, l * 16:(l + 1) * 16]
                q = self.DER[:, d0 + 32:d0 + 48]
                self.ts(self.DVE, self.DER, q, E1, e_ap, -1.0 / 6.0, ALU.mult)
                for cst in (1.0 / 5.0, -1.0 / 4.0, 1.0 / 3.0, -1.0 / 2.0, 1.0):
                    self.stt(self.DER, q, self.DER, q, cst, E1, e_ap, ALU.add, ALU.mult)
                self.ts(self.DVE, self.DER, self.DER[:, d0 + 48:d0 + 64], self.DER, q, -4.0, ALU.mult)
                self.ts(self.DVE, self.DER, q, self.DER, q, -8.0, ALU.mult)
        self.barrier()

    def dv(self, buf, c0, n):
        return buf.t[:, c0:c0 + n].rearrange("(ft p) t -> p ft t", p=128)

    def layer(self, l):
        x_src = self.xT if l == 0 else self.d_x2
        sa = getattr(self, "stop_after", 9)
        self.pass1(l, x_src)
        self.barrier()
        if sa <= 1:
            return
        self.pass2(l)
        self.barrier()
        if sa <= 2:
            return
        self.pass3a(l)
        self.barrier()
        if sa <= 3:
            return
        self.pass3b(l, x_src)
        self.barrier()
        if sa <= 4:
            return
        last = (l == self.depth - 1)
        self.pass4(l, self.outT if last else self.d_x2, last)
        self.barrier()

    def norm_stage(self, X, SQ, RS, LNT, H, gcol, n):
        self.norm_stats(X, SQ, RS, LNT, n)
        self.norm_apply(X, RS, H, gcol, n)

    def norm_apply(self, X, RS, H, gcol, n):
        for ft in range(FT):
            self.stt(H, H[:, ft, 0:n], X, X[:, ft, 0:n], self.SPK[:, gcol + ft:gcol + ft + 1], RS, RS[:, 0:n],
                     ALU.mult, ALU.mult, extra_reads=(self.SPK,))

    def norm_stats(self, X, SQ, RS, LNT, n):
        self.act(SQ, SQ[:, :, 0:n], X, X[:, :, 0:n], AF.Square)
        ps = self.next_ps()
        for ft in range(FT):
            self.mm(ps, ps[:, 0:n], self.ONESM, self.ONESM[:, :], SQ, SQ[:, ft, 0:n], ft == 0, ft == FT - 1)
        self.act(LNT, LNT[:, 0:n], ps, ps[:, 0:n], AF.Ln, bias=self.EPSB[:, 0:1], extra_reads=(self.EPSB,))
        self.act(RS, RS[:, 0:n], LNT, LNT[:, 0:n], AF.Exp, scale=-0.5)

    def pass1(self, l, x_src):
        nch = self.nch
        b0 = l * PL
        with ExitStack() as st:
            W = self.sb(st, "p1_W", [128, FT, IN_COLS], BF16)
            if l == 0:
                with ExitStack() as t1:
                    STG = [self.sb(t1, "p1_STG%d" % i, [128, IN_COLS], F32) for i in range(2)]
                    for kt in range(FT):
                        sg = STG[kt % 2]
                        self.dma(self.SP, sg, sg[:, :], self.w_in, self.w_in.t[0, kt * 128:(kt + 1) * 128, :])
                        half = IN_COLS // 2
                        self.act(W, W[:, kt, 0:half], sg, sg[:, 0:half], AF.Copy)
                        self.copy(self.DVE, W, W[:, kt, half:IN_COLS], sg, sg[:, half:IN_COLS])
                    self.barrier()
            else:
                wsrc = self.b_win[l]
                wv = wsrc.t.rearrange("(kt p) n -> p kt n", p=128)
                for kt in range(FT):
                    self.dma(self.SP, W, W[:, kt, :], wsrc, wv[:, kt, :], accumulate=True)
            self.EPSB = self.sb(st, "p1_eps", [128, 1], F32)
            self.memset(self.DVE, self.EPSB, self.EPSB[:, :], EPS)
            X = [self.sb(st, "p1_X%d" % i, [128, FT, CH], F32) for i in range(2)]
            SQ = self.sb(st, "p1_SQ", [128, FT, CH], BF16)
            LNT = self.sb(st, "p1_LNT", [128, CH], F32)
            RS = self.sb(st, "p1_RS", [128, CH], F32)
            H = [self.sb(st, "p1_H%d" % i, [128, FT, CH], BF16) for i in range(2)]
            Q = self.sb(st, "p1_Q", [128, FT, CH], BF16)
            KK = self.sb(st, "p1_K", [128, 2, CH], BF16)
            V = self.sb(st, "p1_V", [128, 4, 256], BF16)
            XR = self.sb(st, "p1_XR", [128, FT, CH], BF16)
            GR = self.sb(st, "p1_GR", [128, FT, CH], BF16)
            TA = self.sb(st, "p1_TA", [128, FT, CH], BF16)
            TB = self.sb(st, "p1_TB", [128, FT, CH], BF16)
            GT = [self.sb(st, "p1_GT%d" % i, [128, CH], F32) for i in range(2)]

            def load(c):
                self.dma(self.SP, X[c % 2], X[c % 2][:, :, :], x_src, self.dv(x_src, c * CH, CH))

            load(0)
            if nch > 1:
                load(1)
            self.norm_stage(X[0], SQ, RS, LNT, H[0], b0 + C_N1, CH)
            rot = 0
            for c in range(nch):
                if c + 1 < nch:
                    self.norm_stage(X[(c + 1) % 2], SQ, RS, LNT, H[(c + 1) % 2], b0 + C_N1, CH)
                if c + 2 < nch:
                    load(c + 2)
                Hc = H[c % 2]
                c0 = c * CH
                groups = [(Q, OQ, 8, "q"), (KK, OK_, 2, "k"), (XR, OXR, 8, "c"), (GR, OGR, 8, "g"),
                          (TA, OGA, 8, "t"), (TB, OGB, 8, "t")]
                for (dst, off, nt, kind) in groups:
                    for j in range(nt):
                        ps = self.next_ps()
                        for kt in range(FT):
                            self.mm(ps, ps[:, :], W, W[:, kt, off + j * 128:off + (j + 1) * 128], Hc, Hc[:, kt, :],
                                    kt == 0, kt == FT - 1)
                        if kind == "t":
                            self.act(dst, dst[:, j, :], ps, ps[:, :], AF.Tanh, scale=0.5)
                        elif kind == "q":
                            self.ts(self.DVE, dst, dst[:, j, :], ps, ps[:, :], 1.0 / math.sqrt(128.0), ALU.mult)
                        elif kind == "g":
                            g1 = GT[j % 2]
                            self.act(g1, g1[:, :], ps, ps[:, :], AF.Square, scale=math.sqrt(0.044715))
                            self.stt(g1, g1[:, :], g1, g1[:, :], 1.0, ps, ps[:, :], ALU.add, ALU.mult)
                            self.act(g1, g1[:, :], g1, g1[:, :], AF.Tanh, scale=GELU_C)
                            self.stt(dst, dst[:, j, :], g1, g1[:, :], 1.0, ps, ps[:, :], ALU.add, ALU.mult)
                        else:
                            rot += 1
                            if rot % 2 == 0:
                                self.act(dst, dst[:, j, :], ps, ps[:, :], AF.Copy)
                            else:
                                self.copy(self.DVE, dst, dst[:, j, :], ps, ps[:, :])
                for s in range(4):
                    ps = self.next_ps()
                    for kt in range(FT):
                        self.mm(ps, ps[:, 0:256], Hc, Hc[:, kt, s * 128:(s + 1) * 128], W, W[:, kt, OV:OV + 256],
                                kt == 0, kt == FT - 1)
                    self.copy(self.DVE, V, V[:, s, :], ps, ps[:, 0:256])
                for (dst, dd) in ((Q, self.d_q), (XR, self.d_xr), (GR, self.d_gr), (TA, self.d_ta), (TB, self.d_tb)):
                    self.dma(self.SP, dd, self.dv(dd, c0, CH), dst, dst[:, :, :])
                self.dma(self.SP, self.d_k, self.d_k.t[:, c0:c0 + CH].rearrange("(g p) t -> p g t", p=128), KK, KK[:, :, :])
                self.dma(self.SP, self.d_v, self.d_v.t[c0:c0 + CH, :].rearrange("(s p) d -> p s d", p=128), V, V[:, :, :])

    def lru_dir(self, l, d, XCb, XCm, WR, WI, TRr, TIr, A, A2, IXC, HS, prev_carry, n, reverse):
        d0 = l * 64
        for ft in range(FT):
            psr = self.next_ps()
            self.mm(psr, psr[:, 0:n], WR, WR[:, ft, :], XCb, XCb[:, ft, 0:n], True, True)
            psi = self.next_ps()
            self.mm(psi, psi[:, 0:n], WI, WI[:, ft, :], XCb, XCb[:, ft, 0:n], True, True)
            TR = TRr[ft % 2]
            TI = TIr[ft % 2]
            cr = d0 + d * 8 + ft
            ci = d0 + 16 + d * 8 + ft
            ck = d0 + 32 + d * 8 + ft
            ch = d0 + 48 + d * 8 + ft
            self.act(TR, TR[:, 0:n], psr, psr[:, 0:n], AF.Tanh, bias=self.DER[:, cr:cr + 1], scale=0.5,
                     extra_reads=(self.DER,))
            self.act(TI, TI[:, 0:n], psi, psi[:, 0:n], AF.Tanh, bias=self.DER[:, ci:ci + 1], scale=0.5,
                     extra_reads=(self.DER,))
            self.act(A, A[:, ft, 0:n], TR, TR[:, 0:n], AF.Exp, bias=self.DER[:, ch:ch + 1], scale=self.DER[:, ch:ch + 1],
                     extra_reads=(self.DER,))
            self.act(A2, A2[:, ft, 0:n], TR, TR[:, 0:n], AF.Exp, bias=self.DER[:, ck:ck + 1], scale=self.DER[:, ck:ck + 1],
                     extra_reads=(self.DER,))
            self.stt(IXC, IXC[:, ft, 0:n], TI, TI[:, 0:n], 1.0, XCm, XCm[:, ft, 0:n], ALU.add, ALU.mult)
        self.act(A2, A2[:, :, 0:n], A2, A2[:, :, 0:n], AF.Sqrt, bias=self.QTR[:, 0:1], scale=-0.25, extra_reads=(self.QTR,))
        self.tt(self.DVE, IXC, IXC[:, :, 0:n], A2, A2[:, :, 0:n], IXC, IXC[:, :, 0:n], ALU.mult)
        for ft in range(FT):
            if prev_carry is None:
                init = 0.0
                er = ()
            elif len(prev_carry) == 1:
                pb = prev_carry[0]
                init = pb[:, ft:ft + 1]
                er = (pb,)
            else:
                pb, col = prev_carry
                init = pb[:, ft, col:col + 1]
                er = (pb,)
            if reverse:
                o_ap, a_ap, b_ap = HS[:, ft, 0:n][:, ::-1], A[:, ft, 0:n][:, ::-1], IXC[:, ft, 0:n][:, ::-1]
            else:
                o_ap, a_ap, b_ap = HS[:, ft, 0:n], A[:, ft, 0:n], IXC[:, ft, 0:n]
            self.op(self.DVE, lambda o_ap=o_ap, a_ap=a_ap, b_ap=b_ap, init=init: self.nc.vector.tensor_tensor_scan(
                out=o_ap, data0=a_ap, data1=b_ap, initial=init, op0=ALU.mult, op1=ALU.add),
                reads=(A, IXC) + er, writes=(HS,))

    def load_gate_w(self, st, l, d, tag):
        WR = self.sb(st, tag + "_WR", [128, 8, 128], BF16)
        WI = self.sb(st, tag + "_WI", [128, 8, 128], BF16)
        self.dma(self.SP, WR, WR[:, :, :], self.b_wrg[l], self.b_wrg[l].t[d].rearrange("n h k -> h n k"))
        self.dma(self.SP, WI, WI[:, :, :], self.b_wig[l], self.b_wig[l].t[d].rearrange("n h k -> h n k"))
        return WR, WI

    def pass2(self, l):
        nch = self.nch
        S = self.seq
        b0 = l * PL
        with ExitStack() as st:
            WR, WI = self.load_gate_w(st, l, 0, "p2")
            DGH = self.sb(st, "p2_DGH", [128, 32, 128], BF16)
            DGL = self.sb(st, "p2_DGL", [128, 32, 128], BF16)
            with ExitStack() as t2:
                DGF = self.sb(t2, "p2_DGF", [128, 32, 128], F32)
                DGF2 = self.sb(t2, "p2_DGF2", [128, 32, 128], F32)
                cw = b0 + C_CW
                self.op(self.DVE, lambda: self.nc.vector.tensor_tensor(
                    out=DGF[:, :, :], in0=self.IDF[:, :].unsqueeze(1).to_broadcast([128, 32, 128]),
                    in1=self.SPK[:, cw:cw + 32].unsqueeze(2).to_broadcast([128, 32, 128]), op=ALU.mult),
                    reads=(self.IDF, self.SPK), writes=(DGF,))
                self.copy(self.DVE, DGH, DGH[:, :, :], DGF, DGF[:, :, :])
                self.copy(self.DVE, DGF2, DGF2[:, :, :], DGH, DGH[:, :, :])
                self.tt(self.DVE, DGF, DGF[:, :, :], DGF, DGF[:, :, :], DGF2, DGF2[:, :, :], ALU.subtract)
                self.copy(self.DVE, DGL, DGL[:, :, :], DGF, DGF[:, :, :])
                self.barrier()
            XRW = [self.sb(st, "p2_XRW%d" % i, [128, FT, CH + 3], BF16) for i in range(2)]
            XCb = [self.sb(st, "p2_XCb%d" % i, [128, FT, CH], BF16) for i in range(2)]
            TRr = [self.sb(st, "p2_TR%d" % i, [128, CH], F32) for i in range(2)]
            TIr = [self.sb(st, "p2_TI%d" % i, [128, CH], F32) for i in range(2)]
            A = [self.sb(st, "p2_A%d" % i, [128, FT, CH], F32) for i in range(2)]
            A2 = [self.sb(st, "p2_A2%d" % i, [128, FT, CH], F32) for i in range(2)]
            IXC = [self.sb(st, "p2_IXC%d" % i, [128, FT, CH], F32) for i in range(2)]
            HS = [self.sb(st, "p2_HS%d" % i, [128, FT, CH], F32) for i in range(2)]
            HSb = self.sb(st, "p2_HSb", [128, FT, CH], BF16)

            def load(c):
                B = XRW[c % 2]
                lo = c * CH - 2
                hi = c * CH + CH + 1
                clo = 0
                if lo < 0:
                    self.memset(self.DVE, B, B[:, :, 0:2], 0.0)
                    clo = -lo
                    lo = 0
                chi = CH + 3
                if hi > S:
                    self.memset(self.DVE, B, B[:, :, CH + 2:CH + 3], 0.0)
                    chi -= hi - S
                    hi = S
                self.dma(self.SP, B, B[:, :, clo:chi], self.d_xr, self.dv(self.d_xr, lo, hi - lo), accumulate=True)

            def conv(c):
                B = XRW[c % 2]
                for ft in range(FT):
                    ps = self.next_ps()
                    k = 0
                    for j in range(4):
                        for DG in (DGH, DGL):
                            self.mm(ps, ps[:, :], DG, DG[:, j * 8 + ft, :], B, B[:, ft, j:j + CH], k == 0, k == 7)
                            k += 1
                    cb = self.SPK[:, b0 + C_CB + ft:b0 + C_CB + ft + 1]
                    self.ts(self.DVE, XCb[c % 2], XCb[c % 2][:, ft, :], ps, ps[:, :], cb, ALU.add, extra_reads=(self.SPK,))
                self.dma(self.SP, self.d_xc, self.dv(self.d_xc, c * CH, CH), XCb[c % 2], XCb[c % 2][:, :, :])

            load(0)
            if nch > 1:
                load(1)
            conv(0)
            for c in range(nch):
                if c + 1 < nch:
                    conv(c + 1)
                if c + 2 < nch:
                    load(c + 2)
                p = c % 2
                prev = None if c == 0 else (HS[1 - p], CH - 1)
                self.lru_dir(l, 0, XCb[p], XCb[p], WR, WI, TRr, TIr, A[p], A2[p], IXC[p], HS[p], prev, CH, False)
                self.copy(self.DVE, HSb, HSb[:, :, :], HS[p], HS[p][:, :, :])
                self.dma(self.SP, self.d_hf, self.dv(self.d_hf, c * CH, CH), HSb, HSb[:, :, :])

    def pass3a(self, l):
        nch = self.nch
        with ExitStack() as st:
            WR, WI = self.load_gate_w(st, l, 1, "p3a")
            WOL = self.sb(st, "p3a_WOL", [128, FT, D], BF16)
            self.dma(self.SP, WOL, WOL[:, :, :], self.b_wol[l], self.b_wol[l].t.rearrange("(kt p) n -> p kt n", p=128))
            XCb = [self.sb(st, "p3a_XCb%d" % i, [128, FT, CH], BF16) for i in range(2)]
            HFb = [self.sb(st, "p3a_HFb", [128, FT, CH], BF16)] * 2
            GR = [self.sb(st, "p3a_GR", [128, FT, CH], BF16)] * 2
            TB = [self.sb(st, "p3a_TB", [128, FT, CH], BF16)] * 2
            TRr = [self.sb(st, "p3a_TR%d" % i, [128, CH], F32) for i in range(2)]
            TIr = [self.sb(st, "p3a_TI%d" % i, [128, CH], F32) for i in range(2)]
            A = [self.sb(st, "p3a_A%d" % i, [128, FT, CH], F32) for i in range(2)]
            A2 = [self.sb(st, "p3a_A2%d" % i, [128, FT, CH], F32) for i in range(2)]
            IXC = [self.sb(st, "p3a_IXC%d" % i, [128, FT, CH], F32) for i in range(2)]
            HS = [self.sb(st, "p3a_HS", [128, FT, CH], F32)] * 2
            CAR = self.sb(st, "p3a_CAR", [128, FT], F32)
            YB = self.sb(st, "p3a_YB", [128, FT, CH], BF16)
            MB = self.sb(st, "p3a_MB", [128, FT, CH], BF16)

            def load(c):
                p = c % 2
                c0 = c * CH
                self.dma(self.SP, XCb[p], XCb[p][:, :, :], self.d_xc, self.dv(self.d_xc, c0, CH))

            def load_hf(c):
                self.dma(self.SP, HFb[0], HFb[0][:, :, :], self.d_hf, self.dv(self.d_hf, c * CH, CH))

            def load_gr(c):
                self.dma(self.SP, GR[0], GR[0][:, :, :], self.d_gr, self.dv(self.d_gr, c * CH, CH))

            def load_tb(c):
                self.dma(self.SP, TB[0], TB[0][:, :, :], self.d_tb, self.dv(self.d_tb, c * CH, CH))

            order = list(range(nch - 1, -1, -1))
            load(order[0])
            load_hf(order[0])
            load_gr(order[0])
            load_tb(order[0])
            for i, c in enumerate(order):
                if i + 1 < len(order):
                    load(order[i + 1])
                p = c % 2
                prev = None if i == 0 else (CAR,)
                self.lru_dir(l, 1, XCb[p], XCb[p], WR, WI, TRr, TIr, A[p], A2[p], IXC[p], HS[p], prev, CH, True)
                self.copy(self.DVE, CAR, CAR[:, :], HS[p], HS[p][:, :, 0])
                self.tt(self.DVE, IXC[p], IXC[p][:, :, :], HS[p], HS[p][:, :, :], HFb[p], HFb[p][:, :, :], ALU.add)
                if i + 1 < len(order):
                    load_hf(order[i + 1])
                self.tt(self.DVE, YB, YB[:, :, :], IXC[p], IXC[p][:, :, :], GR[0], GR[0][:, :, :], ALU.mult)
                if i + 1 < len(order):
                    load_gr(order[i + 1])
                if self.debug:
                    self.dma(self.SP, self.d_hb, self.dv(self.d_hb, c * CH, CH), HS[p], HS[p][:, :, :])
                    self.dma(self.SP, self.d_yb, self.dv(self.d_yb, c * CH, CH), YB, YB[:, :, :])
                for j in range(FT):
                    ps = self.next_ps()
                    for kt in range(FT):
                        self.mm(ps, ps[:, :], WOL, WOL[:, kt, j * 128:(j + 1) * 128], YB, YB[:, kt, :], kt == 0, kt == FT - 1)
                    self.stt(MB, MB[:, j, :], TB[p], TB[p][:, j, :], 1.0, ps, ps[:, :], ALU.add, ALU.mult)
                if i + 1 < len(order):
                    load_tb(order[i + 1])
                self.dma(self.SP, self.d_mb, self.dv(self.d_mb, c * CH, CH), MB, MB[:, :, :])

    def pass3b(self, l, x_src):
        nch = self.nch
        S = self.seq
        nblk = S // 128
        with ExitStack() as st:
            WOA = self.sb(st, "p3b_WOA", [128, FT, D], BF16)
            WOUT = self.sb(st, "p3b_WOUT", [128, FT, D], BF16)
            self.dma(self.SP, WOA, WOA[:, :, :], self.b_woa[l], self.b_woa[l].t.rearrange("(kt p) n -> p kt n", p=128))
            self.dma(self.SP, WOUT, WOUT[:, :, :], self.b_wout[l], self.b_wout[l].t.rearrange("(kt p) n -> p kt n", p=128))
            self.BTH = self.sb(st, "p3b_BTH", [128, 3, 8, 128], BF16)
            self.BTL = self.sb(st, "p3b_BTL", [128, 3, 8, 128], BF16)
            self.dma(self.SP, self.BTH, self.BTH[:, :, :, :], self.d_bth, self.d_bth.t[:, :, :, :])
            self.dma(self.SP, self.BTL, self.BTL[:, :, :, :], self.d_btl, self.d_btl.t[:, :, :, :])
            ESK = self.sb(st, "p3b_ES", [128, 1024], F32)
            self.dma(self.SP, ESK, ESK[:, :], self.sinkrep, self.sinkrep.t[:, l, :])
            self.act(ESK, ESK[:, :], ESK, ESK[:, :], AF.Exp)
            QT = [self.sb(st, "p3b_QT%d" % i, [128, FT, CH], BF16) for i in range(2)]
            KW = [self.sb(st, "p3b_KW%d" % i, [128, 2, CH + 256], BF16) for i in range(2)]
            VW = [self.sb(st, "p3b_VW%d" % i, [128, 6, 256], BF16) for i in range(2)]
            TA = [self.sb(st, "p3b_TA%d" % i, [128, FT, CH], BF16) for i in range(2)]
            MBv = [self.sb(st, "p3b_MB%d" % i, [128, FT, CH], BF16) for i in range(2)]
            X = [self.sb(st, "p3b_X%d" % i, [128, FT, CH], F32) for i in range(2)]
            PT = [self.sb(st, "p3b_PT%d" % i, [128, 512], BF16) for i in range(6)]
            self.pti = 0
            DEN = [self.sb(st, "p3b_DEN%d" % i, [128, 512], F32) for i in range(2)]
            AO = self.sb(st, "p3b_AO", [128, FT, CH], BF16)
            M1 = [self.sb(st, "p3b_M1%d" % i, [128, 512], F32) for i in range(2)]
            MIX = self.sb(st, "p3b_MIX", [128, FT, CH], BF16)

            def load(c):
                p = c % 2
                c0 = c * CH
                self.dma(self.SP, QT[p], QT[p][:, :, :], self.d_q, self.dv(self.d_q, c0, CH))
                lo = max(0, c0 - 128)
                hi = min(S, c0 + CH + 128)
                off = lo - (c0 - 128)
                self.dma(self.SP, KW[p], KW[p][:, :, off:off + hi - lo], self.d_k,
                         self.d_k.t[:, lo:hi].rearrange("(g p) t -> p g t", p=128))
                self.dma(self.SP, VW[p], VW[p][:, off // 128:(off + hi - lo) // 128, :], self.d_v,
                         self.d_v.t[lo:hi, :].rearrange("(s p) d -> p s d", p=128))
                self.dma(self.SP, TA[p], TA[p][:, :, :], self.d_ta, self.dv(self.d_ta, c0, CH))
                self.dma(self.SP, MBv[p], MBv[p][:, :, :], self.d_mb, self.dv(self.d_mb, c0, CH))
                self.dma(self.SP, X[p], X[p][:, :, :], x_src, self.dv(x_src, c0, CH))

            load(0)
            for c in range(nch):
                if c + 1 < nch:
                    load(c + 1)
                p = c % 2
                groups = [(qb, g) for qb in range(4) for g in range(2)]

                def S1(n):
                    qb, g = groups[n]
                    ib = c * 4 + qb
                    jbs = [jb for jb in (ib - 1, ib, ib + 1) if 0 <= jb < nblk]
                    pts = []
                    for jb in jbs:
                        w = jb - (c * 4 - 1)
                        rel = jb - ib + 1
                        pss = self.next_ps()
                        pv = pss[:, :].rearrange("p (h q) -> p h q", h=4)
                        self.mm(pss, pv, KW[p], KW[p][:, g, w * 128:(w + 1) * 128],
                                QT[p], QT[p][:, 4 * g:4 * g + 4, qb * 128:(qb + 1) * 128], True, False)
                        self.mm(pss, pv, self.IDENT, self.IDENT[:, :], self.BTH, self.BTH[:, rel, 4 * g:4 * g + 4, :], False, False)
                        self.mm(pss, pv, self.IDENT, self.IDENT[:, :], self.BTL, self.BTL[:, rel, 4 * g:4 * g + 4, :], False, True)
                        P_ = PT[self.pti % len(PT)]
                        self.pti += 1
                        self.act(P_, P_[:, :], pss, pss[:, :], AF.Exp)
                        pts.append((P_, w))
                    return pts

                def S2(n, pts):
                    qb, g = groups[n]
                    pso = self.next_ps()
                    psd = self.next_ps()
                    for k, (P_, w) in enumerate(pts):
                        self.mm(pso, pso[:, :], VW[p], VW[p][:, w, g * 128:(g + 1) * 128], P_, P_[:, :],
                                k == 0, k == len(pts) - 1)
                    for k, (P_, w) in enumerate(pts):
                        self.mm(psd, psd[:, :], self.ONES, self.ONES[:, :], P_, P_[:, :], k == 0, k == len(pts) - 1)
                    Dn = DEN[n % 2]
                    self.tt(self.DVE, Dn, Dn[:, :], psd, psd[:, :], ESK, ESK[:, g * 512:(g + 1) * 512], ALU.add)
                    self.act(Dn, Dn[:, :], Dn, Dn[:, :], AF.Ln)
                    self.act(Dn, Dn[:, :], Dn, Dn[:, :], AF.Exp, scale=-1.0)
                    self.tt(self.DVE, AO, AO[:, 4 * g:4 * g + 4, qb * 128:(qb + 1) * 128],
                            pso, pso[:, :].rearrange("p (h q) -> p h q", h=4),
                            Dn, Dn[:, :].rearrange("p (h q) -> p h q", h=4), ALU.mult)

                pend = S1(0)
                for n in range(len(groups)):
                    nxt = S1(n + 1) if n + 1 < len(groups) else None
                    S2(n, pend)
                    pend = nxt
                if self.debug:
                    self.dma(self.SP, self.d_ao, self.dv(self.d_ao, c * CH, CH), AO, AO[:, :, :])
                for j in range(FT):
                    ps = self.next_ps()
                    for kt in range(FT):
                        self.mm(ps, ps[:, :], WOA, WOA[:, kt, j * 128:(j + 1) * 128], AO, AO[:, kt, :], kt == 0, kt == FT - 1)
                    m1 = M1[j % 2]
                    self.stt(m1, m1[:, :], TA[p], TA[p][:, j, :], 1.0, ps, ps[:, :], ALU.add, ALU.mult)
                    self.stt(MIX, MIX[:, j, :], MBv[p], MBv[p][:, j, :], 0.5, m1, m1[:, :], ALU.mult, ALU.add)
                for j in range(FT):
                    ps = self.next_ps()
                    for kt in range(FT):
                        self.mm(ps, ps[:, :], WOUT, WOUT[:, kt, j * 128:(j + 1) * 128], MIX, MIX[:, kt, :], kt == 0, kt == FT - 1)
                    self.stt(X[p], X[p][:, j, :], ps, ps[:, :], 0.5, X[p], X[p][:, j, :], ALU.mult, ALU.add)
                self.dma(self.SP, self.d_x1, self.dv(self.d_x1, c * CH, CH), X[p], X[p][:, :, :])

    def pass4(self, l, dst, last):
        nch = self.nch
        b0 = l * PL
        with ExitStack() as st:
            WUP = self.sb(st, "p4_WUP", [128, FT, DFF], BF16)
            wv = self.b_wup[l].t.rearrange("(kt p) n -> p kt n", p=128)
            for kt in range(FT):
                self.dma(self.SP, WUP, WUP[:, kt, :], self.b_wup[l], wv[:, kt, :], accumulate=True)
            WDN = [self.sb(st, "p4_WDN%d" % i, [128, FT, D], BF16) for i in range(2)]
            self.EPSB = self.sb(st, "p4_eps", [128, 1], F32)
            self.memset(self.DVE, self.EPSB, self.EPSB[:, :], EPS)
            X = [self.sb(st, "p4_X%d" % i, [128, FT, CH], F32) for i in range(2)]
            SQ = self.sb(st, "p4_SQ", [128, FT, CH], BF16)
            LNT = self.sb(st, "p4_LNT", [128, CH], F32)
            RS = self.sb(st, "p4_RS", [128, CH], F32)
            H = [self.sb(st, "p4_H", [128, FT, CH], BF16)] * 2
            HID = self.sb(st, "p4_HID", [128, 32, CH], BF16)
            RL = [self.sb(st, "p4_RL%d" % i, [128, CH], F32) for i in range(2)]

            def load(c):
                self.dma(self.SP, X[c % 2], X[c % 2][:, :, :], self.d_x1, self.dv(self.d_x1, c * CH, CH))

            wdi = 0
            load(0)
            if nch > 1:
                load(1)
            self.norm_stage(X[0], SQ, RS, LNT, H[0], b0 + C_N2, CH)
            for c in range(nch):
                Hc = H[c % 2]
                Xc = X[c % 2]
                for j in range(32):
                    if j == 12 and c + 1 < nch:
                        self.norm_stats(X[(c + 1) % 2], SQ, RS, LNT, CH)
                    ps = self.next_ps()
                    for kt in range(FT):
                        self.mm(ps, ps[:, :], WUP, WUP[:, kt, j * 128:(j + 1) * 128], Hc, Hc[:, kt, :], kt == 0, kt == FT - 1)
                    rl = RL[j % 2]
                    self.act(rl, rl[:, :], ps, ps[:, :], AF.Relu)
                    self.tt(self.DVE, HID, HID[:, j, :], rl, rl[:, :], rl, rl[:, :], ALU.mult)
                if c + 1 < nch:
                    self.norm_apply(X[(c + 1) % 2], RS, H[(c + 1) % 2], b0 + C_N2, CH)
                banks = [self.next_ps() for _ in range(8)]
                for s in range(4):
                    Wd = WDN[wdi % 2]
                    wdi += 1
                    self.dma(self.SP, Wd, Wd[:, :, :], self.b_wdn[l],
                             self.b_wdn[l].t[s * 1024:(s + 1) * 1024, :].rearrange("(kt p) n -> p kt n", p=128))
                    for j in range(FT):
                        for kt in range(FT):
                            self.mm(banks[j], banks[j][:, :], Wd, Wd[:, kt, j * 128:(j + 1) * 128], HID, HID[:, s * 8 + kt, :],
                                    s == 0 and kt == 0, s == 3 and kt == FT - 1, signal=(kt == FT - 1))
                for j in range(FT):
                    self.tt(self.DVE, Xc, Xc[:, j, :], banks[j], banks[j][:, :], Xc, Xc[:, j, :], ALU.add)
                if not last:
                    self.dma(self.SP, dst, self.dv(dst, c * CH, CH), Xc, Xc[:, :, :])
                else:
                    gcol = self.depth * PL
                    self.act(SQ, SQ[:, :, :], Xc, Xc[:, :, :], AF.Square)
                    ps = self.next_ps()
                    for ft in range(FT):
                        self.mm(ps, ps[:, :], self.ONESM, self.ONESM[:, :], SQ, SQ[:, ft, :], ft == 0, ft == FT - 1)
                    self.act(LNT, LNT[:, :], ps, ps[:, :], AF.Ln, bias=self.EPSB[:, 0:1], extra_reads=(self.EPSB,))
                    self.act(RS, RS[:, :], LNT, LNT[:, :], AF.Exp, scale=-0.5)
                    for ft in range(FT):
                        self.stt(Xc, Xc[:, ft, :], Xc, Xc[:, ft, :], self.SPK[:, gcol + ft:gcol + ft + 1], RS, RS[:, :],
                                 ALU.mult, ALU.mult, extra_reads=(self.SPK,))
                    self.dma(self.SP, dst, self.dv(dst, c * CH, CH), Xc, Xc[:, :, :])
                if c + 2 < nch:
                    load(c + 2)


def _t5_bucket_np(rel):
    half = 16
    max_exact = 8
    n = np.abs(rel)
    large = max_exact + (np.log(np.maximum(n, 1).astype(np.float32) / np.float32(max_exact))
                         / np.float32(math.log(128 / max_exact)) * np.float32(half - max_exact)).astype(np.int32)
    large = np.minimum(large, half - 1)
    return np.where(rel > 0, half, 0) + np.where(n < max_exact, n, large)


def host_prep(inputs, depth=DEPTH):
    L = depth
    f32 = np.float32
    spk = np.zeros((128, L * PL + 8), f32)

    def pm(v):
        return np.ascontiguousarray(np.asarray(v, f32).reshape(8, 128).T)

    for l in range(L):
        b0 = l * PL
        spk[:, b0 + C_N1:b0 + C_N1 + 8] = pm(inputs["norm1_g"][l])
        spk[:, b0 + C_N2:b0 + C_N2 + 8] = pm(inputs["norm2_g"][l])
        for j in range(4):
            spk[:, b0 + C_CW + j * 8:b0 + C_CW + j * 8 + 8] = pm(inputs["conv_w"][l, j])
        spk[:, b0 + C_CB:b0 + C_CB + 8] = pm(inputs["conv_b"][l])
        for d in range(2):
            spk[:, b0 + C_BR + d * 8:b0 + C_BR + d * 8 + 8] = pm(inputs["b_rgate"][l, d])
            spk[:, b0 + C_BI + d * 8:b0 + C_BI + d * 8 + 8] = pm(inputs["b_igate"][l, d])
            spk[:, b0 + C_LAM + d * 8:b0 + C_LAM + d * 8 + 8] = pm(inputs["lru_lambda"][l, d])
    spk[:, L * PL:L * PL + 8] = pm(inputs["final_norm_g"])
    sink = np.asarray(inputs["attn_sink"], f32)[:L]
    sinkrep = np.ascontiguousarray(np.broadcast_to(np.repeat(sink, 128, axis=1)[None], (128, L, 1024))).astype(f32)
    j = np.arange(128)[:, None, None]
    r = np.arange(3)[None, :, None]
    q = np.arange(128)[None, None, :]
    rel = ((r - 1) * 128 + j - q).astype(np.int32)
    bkt = _t5_bucket_np(rel)
    rb = np.asarray(inputs["rel_bias"], f32)
    biasg = np.ascontiguousarray(np.transpose(rb[bkt], (0, 1, 3, 2))).astype(f32)
    maskc = np.where(np.abs(rel) <= 128, 0.0, MASK_NEG).astype(f32)
    maskc = np.ascontiguousarray(np.broadcast_to(maskc[:, :, None, :], (128, 3, 8, 128))).astype(f32)
    shared = {
        "w_in": np.ascontiguousarray(inputs["w_in"][:L], f32),
        "w_o_attn": np.ascontiguousarray(inputs["w_o_attn"][:L], f32),
        "w_o_lru": np.ascontiguousarray(inputs["w_o_lru"][:L], f32),
        "w_out": np.ascontiguousarray(inputs["w_out"][:L], f32),
        "w_mlp_up": np.ascontiguousarray(inputs["w_mlp_up"][:L], f32),
        "w_mlp_down": np.ascontiguousarray(inputs["w_mlp_down"][:L], f32),
        "w_rgate": np.ascontiguousarray(inputs["w_rgate"][:L], f32),
        "w_igate": np.ascontiguousarray(inputs["w_igate"][:L], f32),
        "spk": spk, "sinkrep": sinkrep, "biasg": biasg, "maskc": maskc,
        "identc": np.eye(128, dtype=f32),
    }
    return shared


def kernel(**inputs):
    x = np.asarray(inputs["x"], np.float32)
    B, S, _ = x.shape
    shared = host_prep(inputs, DEPTH)
    k = K(seq=S, depth=DEPTH)
    nc = k.build()
    in_maps = []
    for b in range(B):
        m = dict(shared)
        m["xT"] = np.ascontiguousarray(x[b].T)
        in_maps.append(m)
    res = run_bass_kernel_spmd(nc, in_maps, core_ids=list(range(B)))
    out = np.empty((B, S, D), np.float32)
    for b in range(B):
        out[b] = np.asarray(res.results[b]["outT"]).T
    return out
```

```python
import math
from contextlib import ExitStack

import numpy as np
import concourse.bass as bass
import concourse.mybir as mybir
from concourse.bass_utils import run_bass_kernel_spmd

F32 = mybir.dt.float32
BF16 = mybir.dt.bfloat16
AF = mybir.ActivationFunctionType
ALU = mybir.AluOpType

D = 1024
FT = 8
DEPTH = 4
SEQ = 8192
NCORES = 8
CH = 512
IN_COLS = 5632
DFF = 4096
EPS = 1e-6
MASK_NEG = -30000.0
OQ, OK_, OV, OXR, OGR, OGA, OGB = 0, 1024, 1280, 1536, 2560, 3584, 4608
PL = 104
C_N1, C_N2, C_CW, C_CB, C_BR, C_BI, C_LAM = 0, 8, 16, 48, 56, 72, 88
GELU_C = math.sqrt(2.0 / math.pi)


class Sem:
    def __init__(self, h, name):
        self.h = h
        self.name = name
        self.v = 0


class Buf:
    def __init__(self, name, t, dram=False):
        self.name = name
        self.t = t
        self.dram = dram
        self.writers = {}
        self.readers = {}
        self.psum = False

    def __getitem__(self, k):
        return self.t[k]


class Eng:
    def __init__(self, name, e, sem):
        self.name = name
        self.e = e
        self.sem = sem
        self.cnt = 0
        self.seen = {}

    def wait_tok(self, sem, val):
        if sem is self.sem and val > self.cnt:
            return
        if self.seen.get(sem, 0) >= val:
            return
        self.e.wait_ge(sem.h, val)
        self.seen[sem] = val


class K:
    def __init__(self, seq=SEQ, depth=DEPTH, debug=False):
        self.seq = seq
        self.depth = depth
        self.nch = seq // CH
        self.debug = debug
        self.nc = bass.Bass("TRN2", target_bir_lowering=False)
        self.es = ExitStack()
        self.sems = []
        self.dma_sems = {}
        self.n_ins = 0
        self.n_alloc = 0

    def new_sem(self, name):
        h = self.es.enter_context(self.nc.semaphore(name))
        s = Sem(h, name)
        self.sems.append(s)
        return s

    def dram(self, name, shape, dt, kind="Internal"):
        t = self.nc.dram_tensor(name, list(shape), dt, kind=kind)
        return Buf(name, t.ap(), dram=True)

    def sb(self, stack, name, shape, dt):
        self.n_alloc += 1
        t = stack.enter_context(self.nc.sbuf_tensor("%s_%d" % (name, self.n_alloc), list(shape), dt))
        return Buf(name, t)

    def op(self, eng, emit, reads=(), writes=(), signal=True):
        for b in reads:
            for s, v in b.writers.items():
                eng.wait_tok(s, v)
            if b.psum:
                for s, v in b.readers.items():
                    eng.wait_tok(s, v)
        for b in writes:
            for s, v in b.writers.items():
                eng.wait_tok(s, v)
            for s, v in b.readers.items():
                eng.wait_tok(s, v)
        ins = emit()
        self.n_ins += 1
        val = eng.cnt + 1
        if signal:
            ins.then_inc(eng.sem.h, 1)
            eng.cnt = val
        for b in reads:
            if b.readers.get(eng.sem, 0) < val:
                b.readers[eng.sem] = val
        for b in writes:
            b.writers = {eng.sem: val}
            b.readers = {}
        return ins

    def dma(self, q, dst, dst_ap, src, src_ap, accumulate=False, **kw):
        key = dst.name if not dst.dram else (src.name if not src.dram else dst.name)
        sem = self.dma_sems.get(key)
        if sem is None:
            sem = self.new_sem("d%d" % len(self.dma_sems))
            self.dma_sems[key] = sem
        for s, v in src.writers.items():
            q.wait_tok(s, v)
        if not (dst.dram or accumulate):
            for s, v in dst.writers.items():
                q.wait_tok(s, v)
        for s, v in dst.readers.items():
            q.wait_tok(s, v)
        ins = q.e.dma_start(out=dst_ap, in_=src_ap, **kw)
        self.n_ins += 1
        sem.v += 16
        ins.then_inc(sem.h, 16)
        src.readers[sem] = sem.v
        if dst.dram or accumulate:
            dst.writers[sem] = sem.v
            if not dst.dram:
                dst.readers = {}
        else:
            dst.writers = {sem: sem.v}
            dst.readers = {}

    def barrier(self, final=False):
        engs = [self.PE, self.ACT, self.DVE, self.POOL, self.SP]
        skip = set() if final else {sm for k_, sm in self.dma_sems.items() if k_.startswith("b_w")}
        for e in engs:
            for s in self.sems:
                if s in skip:
                    continue
                cur = s.v
                for e2 in engs:
                    if e2.sem is s:
                        cur = e2.cnt
                if cur > 0:
                    e.wait_tok(s, cur)

    def mm(self, ps_buf, out_ap, lhsT_buf, lhsT_ap, rhs_buf, rhs_ap, start, stop, signal=None):
        if signal is None:
            signal = stop
        return self.op(self.PE, lambda: self.nc.tensor.matmul(out_ap, lhsT=lhsT_ap, rhs=rhs_ap, start=start, stop=stop),
                       reads=(lhsT_buf, rhs_buf), writes=(ps_buf,), signal=signal)

    def act(self, out_buf, out_ap, in_buf, in_ap, func, bias=None, scale=None, extra_reads=()):
        kw = {}
        if bias is not None:
            kw["bias"] = bias
        if scale is not None:
            kw["scale"] = scale
        return self.op(self.ACT, lambda: self.nc.scalar.activation(out=out_ap, in_=in_ap, func=func, **kw),
                       reads=(in_buf,) + tuple(extra_reads), writes=(out_buf,))

    def tt(self, eng, out_buf, out_ap, a_buf, a_ap, b_buf, b_ap, op):
        e = eng.e
        return self.op(eng, lambda: e.tensor_tensor(out=out_ap, in0=a_ap, in1=b_ap, op=op),
                       reads=(a_buf, b_buf), writes=(out_buf,))

    def stt(self, out_buf, out_ap, a_buf, a_ap, scalar, b_buf, b_ap, op0, op1, extra_reads=()):
        return self.op(self.DVE, lambda: self.nc.vector.scalar_tensor_tensor(
            out=out_ap, in0=a_ap, scalar=scalar, in1=b_ap, op0=op0, op1=op1),
            reads=(a_buf, b_buf) + tuple(extra_reads), writes=(out_buf,))

    def ts(self, eng, out_buf, out_ap, a_buf, a_ap, s1, op0, s2=None, op1=None, extra_reads=()):
        e = eng.e
        if op1 is None:
            return self.op(eng, lambda: e.tensor_scalar(out=out_ap, in0=a_ap, scalar1=s1, scalar2=None, op0=op0),
                           reads=(a_buf,) + tuple(extra_reads), writes=(out_buf,))
        return self.op(eng, lambda: e.tensor_scalar(out=out_ap, in0=a_ap, scalar1=s1, scalar2=s2, op0=op0, op1=op1),
                       reads=(a_buf,) + tuple(extra_reads), writes=(out_buf,))

    def copy(self, eng, out_buf, out_ap, in_buf, in_ap):
        e = eng.e
        return self.op(eng, lambda: e.tensor_copy(out=out_ap, in_=in_ap), reads=(in_buf,), writes=(out_buf,))

    def memset(self, eng, buf, ap, val):
        e = eng.e
        return self.op(eng, lambda: e.memset(ap, val), reads=(), writes=(buf,))

    def next_ps(self):
        b = self.psum[self.ps_i % len(self.psum)]
        self.ps_i += 1
        return b

    def build(self):
        nc = self.nc
        S = self.seq
        L = self.depth
        es = self.es
        self.PE = Eng("pe", nc.tensor, self.new_sem("s_pe"))
        self.ACT = Eng("act", nc.scalar, self.new_sem("s_act"))
        self.DVE = Eng("dve", nc.vector, self.new_sem("s_dve"))
        self.POOL = Eng("pool", nc.gpsimd, self.new_sem("s_pool"))
        self.SP = Eng("sp", nc.sync, self.new_sem("s_sp"))

        ext = "ExternalInput"
        self.xT = self.dram("xT", [D, S], F32, ext)
        self.w_in = self.dram("w_in", [L, D, IN_COLS], F32, ext)
        self.w_oa = self.dram("w_o_attn", [L, D, D], F32, ext)
        self.w_ol = self.dram("w_o_lru", [L, D, D], F32, ext)
        self.w_out = self.dram("w_out", [L, D, D], F32, ext)
        self.w_up = self.dram("w_mlp_up", [L, D, DFF], F32, ext)
        self.w_dn = self.dram("w_mlp_down", [L, DFF, D], F32, ext)
        self.w_rg = self.dram("w_rgate", [L, 2, 8, 128, 128], F32, ext)
        self.w_ig = self.dram("w_igate", [L, 2, 8, 128, 128], F32, ext)
        self.spk = self.dram("spk", [128, L * PL + 8], F32, ext)
        self.sinkrep = self.dram("sinkrep", [128, L, 1024], F32, ext)
        self.biasg = self.dram("biasg", [128, 3, 8, 128], F32, ext)
        self.maskc = self.dram("maskc", [128, 3, 8, 128], F32, ext)
        self.identc = self.dram("identc", [128, 128], F32, ext)
        self.outT = self.dram("outT", [D, S], F32, "ExternalOutput")
        dk = "ExternalOutput" if self.debug else "Internal"
        self.b_win = [self.dram("b_win%d" % l, [D, IN_COLS], BF16) for l in range(L)]
        self.b_woa = [self.dram("b_woa%d" % l, [D, D], BF16) for l in range(L)]
        self.b_wol = [self.dram("b_wol%d" % l, [D, D], BF16) for l in range(L)]
        self.b_wout = [self.dram("b_wout%d" % l, [D, D], BF16) for l in range(L)]
        self.b_wup = [self.dram("b_wup%d" % l, [D, DFF], BF16) for l in range(L)]
        self.b_wdn = [self.dram("b_wdn%d" % l, [DFF, D], BF16) for l in range(L)]
        self.b_wrg = [self.dram("b_wrg%d" % l, [2, 8, 128, 128], BF16) for l in range(L)]
        self.b_wig = [self.dram("b_wig%d" % l, [2, 8, 128, 128], BF16) for l in range(L)]
        self.d_q = self.dram("d_q", [D, S], BF16, dk)
        self.d_k = self.dram("d_k", [256, S], BF16, dk)
        self.d_v = self.dram("d_v", [S, 256], BF16, dk)
        self.d_xr = self.dram("d_xr", [D, S], BF16, dk)
        self.d_gr = self.dram("d_gr", [D, S], BF16, dk)
        self.d_ta = self.dram("d_ta", [D, S], BF16, dk)
        self.d_tb = self.dram("d_tb", [D, S], BF16, dk)
        self.d_xc = self.dram("d_xc", [D, S], BF16, dk)
        self.d_hf = self.dram("d_hf", [D, S], BF16, dk)
        self.d_mb = self.dram("d_mb", [D, S], BF16, dk)
        self.d_x1 = self.dram("d_x1", [D, S], F32, dk)
        self.d_x2 = self.dram("d_x2", [D, S], F32, dk)
        self.d_bth = self.dram("d_bth", [128, 3, 8, 128], BF16)
        self.d_btl = self.dram("d_btl", [128, 3, 8, 128], BF16)
        if self.debug:
            self.d_hb = self.dram("d_hb", [D, S], F32, dk)
            self.d_yb = self.dram("d_yb", [D, S], BF16, dk)
            self.d_ao = self.dram("d_ao", [D, S], BF16, dk)

        self.psum = []
        for i in range(8):
            t = es.enter_context(nc.psum_tensor("ps%d" % i, [128, 512], F32))
            self.psum.append(Buf("ps%d" % i, t))
            self.psum[-1].psum = True
        self.ps_i = 0

        g = es
        self.SPK = self.sb(g, "SPK", [128, L * PL + 8], F32)
        self.DER = self.sb(g, "DER", [128, L * 64], F32)
        self.ONES = self.sb(g, "ONES", [128, 128], BF16)
        self.ONESM = self.sb(g, "ONESM", [128, 128], BF16)
        self.IDENT = self.sb(g, "IDENT", [128, 128], BF16)
        self.QTR = self.sb(g, "QTR", [128, 1], F32)
        self.IDF = self.sb(g, "IDF", [128, 128], F32)

        self.setup()
        for l in range(L):
            self.layer(l)
        self.barrier(final=True)
        es.close()
        return nc

    def cast_w(self, dst, src_ap, rows, cols):
        r = rows // 128
        s_ap = src_ap.rearrange("(p r) n -> p (r n)", p=128)
        d_ap = dst.t.rearrange("(p r) n -> p (r n)", p=128)
        tot = r * cols
        step = 8192
        srcbuf = self._cast_src
        for c0 in range(0, tot, step):
            c1 = min(tot, c0 + step)
            self.dma(self.POOL, dst, d_ap[:, c0:c1], srcbuf, s_ap[:, c0:c1], max_dma_last_dim=4096)

    def setup(self):
        nc = self.nc
        L = self.depth
        for l in range(L):
            for (dst, src, rows, cols) in ((() if l == 0 else ((self.b_win[l], self.w_in, D, IN_COLS),)) + (
                (self.b_wol[l], self.w_ol, D, D),
                (self.b_woa[l], self.w_oa, D, D),
                (self.b_wout[l], self.w_out, D, D),
                (self.b_wup[l], self.w_up, D, DFF),
                (self.b_wdn[l], self.w_dn, DFF, D),
            )):
                self._cast_src = src
                self.cast_w(dst, src.t[l], rows, cols)
            for (dst, src) in ((self.b_wrg[l], self.w_rg), (self.b_wig[l], self.w_ig)):
                s_ap = src.t[l].rearrange("d n h k -> (d n) (h k)")
                d_ap = dst.t.rearrange("d n h k -> (d n) (h k)")
                self.dma(self.POOL, dst, d_ap, src, s_ap, max_dma_last_dim=4096)
        self.dma(self.SP, self.SPK, self.SPK[:, :], self.spk, self.spk.t[:, :])
        with ExitStack() as st:
            TMP = self.sb(st, "su_tmp", [128, 3, 8, 128], F32)
            TMP2 = self.sb(st, "su_tmp2", [128, 3, 8, 128], F32)
            TMP3 = self.sb(st, "su_tmp3", [128, 3, 8, 128], F32)
            IDF = self.IDF
            self.BTH = self.sb(st, "su_BTH", [128, 3, 8, 128], BF16)
            self.BTL = self.sb(st, "su_BTL", [128, 3, 8, 128], BF16)
            E1 = self.sb(st, "su_e1", [128, L * 16], F32)
            self.memset(self.DVE, self.ONES, self.ONES[:, :], 1.0)
            self.memset(self.DVE, self.ONESM, self.ONESM[:, :], 1.0 / 1024.0)
            self.memset(self.DVE, self.QTR, self.QTR[:, :], 0.25000003)
            self.dma(self.SP, IDF, IDF[:, :], self.identc, self.identc.t[:, :])
            self.copy(self.DVE, self.IDENT, self.IDENT[:, :], IDF, IDF[:, :])
            self.dma(self.SP, TMP, TMP[:, :, :, :], self.biasg, self.biasg.t[:, :, :, :])
            self.dma(self.SP, TMP2, TMP2[:, :, :, :], self.maskc, self.maskc.t[:, :, :, :])
            self.tt(self.DVE, TMP, TMP[:, :, :, :], TMP, TMP[:, :, :, :], TMP2, TMP2[:, :, :, :], ALU.add)
            self.copy(self.DVE, self.BTH, self.BTH[:, :, :, :], TMP, TMP[:, :, :, :])
            self.copy(self.DVE, TMP2, TMP2[:, :, :, :], self.BTH, self.BTH[:, :, :, :])
            self.tt(self.DVE, TMP3, TMP3[:, :, :, :], TMP, TMP[:, :, :, :], TMP2, TMP2[:, :, :, :], ALU.subtract)
            self.copy(self.DVE, self.BTL, self.BTL[:, :, :, :], TMP3, TMP3[:, :, :, :])
            self.dma(self.SP, self.d_bth, self.d_bth.t[:, :, :, :], self.BTH, self.BTH[:, :, :, :])
            self.dma(self.SP, self.d_btl, self.d_btl.t[:, :, :, :], self.BTL, self.BTL[:, :, :, :])
            for l in range(L):
                b0 = l * PL
                d0 = l * 64
                self.ts(self.DVE, self.DER, self.DER[:, d0:d0 + 32], self.SPK, self.SPK[:, b0 + C_BR:b0 + C_BR + 32],
                        0.5, ALU.mult)
                self.act(E1, E1[:, l * 16:(l + 1) * 16], self.SPK, self.SPK[:, b0 + C_LAM:b0 + C_LAM + 16],
                         AF.Exp, scale=-1.0)
            for l in range(L):
                d0 = l * 64
                e_ap = E1[:, l * 16:(l + 1) * 16]
                q = self.DER[:, d0 + 32:d0 + 48]
                self.ts(self.DVE, self.DER, q, E1, e_ap, -1.0 / 6.0, ALU.mult)
                for cst in (1.0 / 5.0, -1.0 / 4.0, 1.0 / 3.0, -1.0 / 2.0, 1.0):
                    self.stt(self.DER, q, self.DER, q, cst, E1, e_ap, ALU.add, ALU.mult)
                self.ts(self.DVE, self.DER, self.DER[:, d0 + 48:d0 + 64], self.DER, q, -4.0, ALU.mult)
                self.ts(self.DVE, self.DER, q, self.DER, q, -8.0, ALU.mult)
        self.barrier()

    def dv(self, buf, c0, n):
        return buf.t[:, c0:c0 + n].rearrange("(ft p) t -> p ft t", p=128)

    def layer(self, l):
        x_src = self.xT if l == 0 else self.d_x2
        sa = getattr(self, "stop_after", 9)
        self.pass1(l, x_src)
        self.barrier()
        if sa <= 1:
            return
        self.pass2(l)
        self.barrier()
        if sa <= 2:
            return
        self.pass3a(l)
        self.barrier()
        if sa <= 3:
            return
        self.pass3b(l, x_src)
        self.barrier()
        if sa <= 4:
            return
        last = (l == self.depth - 1)
        self.pass4(l, self.outT if last else self.d_x2, last)
        self.barrier()

    def norm_stage(self, X, SQ, RS, LNT, H, gcol, n):
        self.norm_stats(X, SQ, RS, LNT, n)
        self.norm_apply(X, RS, H, gcol, n)

    def norm_apply(self, X, RS, H, gcol, n):
        for ft in range(FT):
            self.stt(H, H[:, ft, 0:n], X, X[:, ft, 0:n], self.SPK[:, gcol + ft:gcol + ft + 1], RS, RS[:, 0:n],
                     ALU.mult, ALU.mult, extra_reads=(self.SPK,))

    def norm_stats(self, X, SQ, RS, LNT, n):
        self.norm_sq(X, SQ, n)
        self.norm_rs(SQ, RS, LNT, n)

    def norm_sq(self, X, SQ, n):
        self.act(SQ, SQ[:, :, 0:n], X, X[:, :, 0:n], AF.Square)

    def norm_rs(self, SQ, RS, LNT, n):
        ps = self.next_ps()
        for ft in range(FT):
            self.mm(ps, ps[:, 0:n], self.ONESM, self.ONESM[:, :], SQ, SQ[:, ft, 0:n], ft == 0, ft == FT - 1)
        self.act(LNT, LNT[:, 0:n], ps, ps[:, 0:n], AF.Ln, bias=self.EPSB[:, 0:1], extra_reads=(self.EPSB,))
        self.act(RS, RS[:, 0:n], LNT, LNT[:, 0:n], AF.Exp, scale=-0.5)

    def pass1(self, l, x_src):
        nch = self.nch
        b0 = l * PL
        with ExitStack() as st:
            W = self.sb(st, "p1_W", [128, FT, IN_COLS], BF16)
            if l == 0:
                with ExitStack() as t1:
                    STG = [self.sb(t1, "p1_STG%d" % i, [128, IN_COLS], F32) for i in range(2)]
                    for kt in range(FT):
                        sg = STG[kt % 2]
                        self.dma(self.SP, sg, sg[:, :], self.w_in, self.w_in.t[0, kt * 128:(kt + 1) * 128, :])
                        half = IN_COLS // 2
                        self.act(W, W[:, kt, 0:half], sg, sg[:, 0:half], AF.Copy)
                        self.copy(self.DVE, W, W[:, kt, half:IN_COLS], sg, sg[:, half:IN_COLS])
                    self.barrier()
            else:
                wsrc = self.b_win[l]
                wv = wsrc.t.rearrange("(kt p) n -> p kt n", p=128)
                for kt in range(FT):
                    self.dma(self.SP, W, W[:, kt, :], wsrc, wv[:, kt, :], accumulate=True)
            self.EPSB = self.sb(st, "p1_eps", [128, 1], F32)
            self.memset(self.DVE, self.EPSB, self.EPSB[:, :], EPS)
            X = [self.sb(st, "p1_X%d" % i, [128, FT, CH], F32) for i in range(2)]
            SQ = self.sb(st, "p1_SQ", [128, FT, CH], BF16)
            LNT = self.sb(st, "p1_LNT", [128, CH], F32)
            RS = self.sb(st, "p1_RS", [128, CH], F32)
            H = [self.sb(st, "p1_H%d" % i, [128, FT, CH], BF16) for i in range(2)]
            Q = self.sb(st, "p1_Q", [128, FT, CH], BF16)
            KK = self.sb(st, "p1_K", [128, 2, CH], BF16)
            V = self.sb(st, "p1_V", [128, 4, 256], BF16)
            XR = self.sb(st, "p1_XR", [128, FT, CH], BF16)
            GR = self.sb(st, "p1_GR", [128, FT, CH], BF16)
            TA = self.sb(st, "p1_TA", [128, FT, CH], BF16)
            TB = self.sb(st, "p1_TB", [128, FT, CH], BF16)
            GT = [self.sb(st, "p1_GT%d" % i, [128, CH], F32) for i in range(2)]

            def load(c):
                self.dma(self.SP, X[c % 2], X[c % 2][:, :, :], x_src, self.dv(x_src, c * CH, CH))

            load(0)
            if nch > 1:
                load(1)
            self.norm_stage(X[0], SQ, RS, LNT, H[0], b0 + C_N1, CH)
            rot = 0
            for c in range(nch):
                gi = 0
                Hc = H[c % 2]
                c0 = c * CH
                groups = [(Q, OQ, 8, "q"), (KK, OK_, 2, "k"), (XR, OXR, 8, "c"), (GR, OGR, 8, "g"),
                          (TA, OGA, 8, "t"), (TB, OGB, 8, "t")]
                for (dst, off, nt, kind) in groups:
                    for j in range(nt):
                        gi += 1
                        if gi == 5 and c + 1 < nch:
                            self.norm_sq(X[(c + 1) % 2], SQ, CH)
                        if gi == 20 and c + 1 < nch:
                            self.norm_rs(SQ, RS, LNT, CH)
                            self.norm_apply(X[(c + 1) % 2], RS, H[(c + 1) % 2], b0 + C_N1, CH)
                            if c + 2 < nch:
                                load(c + 2)
                        ps = self.next_ps()
                        for kt in range(FT):
                            self.mm(ps, ps[:, :], W, W[:, kt, off + j * 128:off + (j + 1) * 128], Hc, Hc[:, kt, :],
                                    kt == 0, kt == FT - 1)
                        if kind == "t":
                            self.act(dst, dst[:, j, :], ps, ps[:, :], AF.Tanh, scale=0.5)
                        elif kind == "q":
                            self.ts(self.DVE, dst, dst[:, j, :], ps, ps[:, :], 1.0 / math.sqrt(128.0), ALU.mult)
                        elif kind == "g":
                            g1 = GT[j % 2]
                            self.act(g1, g1[:, :], ps, ps[:, :], AF.Square, scale=math.sqrt(0.044715))
                            self.stt(g1, g1[:, :], g1, g1[:, :], 1.0, ps, ps[:, :], ALU.add, ALU.mult)
                            self.act(g1, g1[:, :], g1, g1[:, :], AF.Tanh, scale=GELU_C)
                            self.stt(dst, dst[:, j, :], g1, g1[:, :], 1.0, ps, ps[:, :], ALU.add, ALU.mult)
                        else:
                            rot += 1
                            if rot % 2 == 0:
                                self.act(dst, dst[:, j, :], ps, ps[:, :], AF.Copy)
                            else:
                                self.copy(self.DVE, dst, dst[:, j, :], ps, ps[:, :])
                for s in range(4):
                    ps = self.next_ps()
                    for kt in range(FT):
                        self.mm(ps, ps[:, 0:256], Hc, Hc[:, kt, s * 128:(s + 1) * 128], W, W[:, kt, OV:OV + 256],
                                kt == 0, kt == FT - 1)
                    self.copy(self.DVE, V, V[:, s, :], ps, ps[:, 0:256])
                for (dst, dd) in ((Q, self.d_q), (XR, self.d_xr), (GR, self.d_gr), (TA, self.d_ta), (TB, self.d_tb)):
                    self.dma(self.SP, dd, self.dv(dd, c0, CH), dst, dst[:, :, :])
                self.dma(self.SP, self.d_k, self.d_k.t[:, c0:c0 + CH].rearrange("(g p) t -> p g t", p=128), KK, KK[:, :, :])
                self.dma(self.SP, self.d_v, self.d_v.t[c0:c0 + CH, :].rearrange("(s p) d -> p s d", p=128), V, V[:, :, :])

    def lru_dir(self, l, d, XCb, XCm, WR, WI, TRr, TIr, A, A2, IXC, HS, prev_carry, n, reverse):
        d0 = l * 64
        for ft in range(FT):
            psr = self.next_ps()
            self.mm(psr, psr[:, 0:n], WR, WR[:, ft, :], XCb, XCb[:, ft, 0:n], True, True)
            psi = self.next_ps()
            self.mm(psi, psi[:, 0:n], WI, WI[:, ft, :], XCb, XCb[:, ft, 0:n], True, True)
            TR = TRr[ft % 2]
            TI = TIr[ft % 2]
            cr = d0 + d * 8 + ft
            ci = d0 + 16 + d * 8 + ft
            ck = d0 + 32 + d * 8 + ft
            ch = d0 + 48 + d * 8 + ft
            self.act(TR, TR[:, 0:n], psr, psr[:, 0:n], AF.Tanh, bias=self.DER[:, cr:cr + 1], scale=0.5,
                     extra_reads=(self.DER,))
            self.act(TI, TI[:, 0:n], psi, psi[:, 0:n], AF.Tanh, bias=self.DER[:, ci:ci + 1], scale=0.5,
                     extra_reads=(self.DER,))
            self.act(A, A[:, ft, 0:n], TR, TR[:, 0:n], AF.Exp, bias=self.DER[:, ch:ch + 1], scale=self.DER[:, ch:ch + 1],
                     extra_reads=(self.DER,))
            self.act(A2, A2[:, ft, 0:n], TR, TR[:, 0:n], AF.Exp, bias=self.DER[:, ck:ck + 1], scale=self.DER[:, ck:ck + 1],
                     extra_reads=(self.DER,))
            self.stt(IXC, IXC[:, ft, 0:n], TI, TI[:, 0:n], 1.0, XCm, XCm[:, ft, 0:n], ALU.add, ALU.mult)
        self.act(A2, A2[:, :, 0:n], A2, A2[:, :, 0:n], AF.Sqrt, bias=self.QTR[:, 0:1], scale=-0.25, extra_reads=(self.QTR,))
        self.tt(self.DVE, IXC, IXC[:, :, 0:n], A2, A2[:, :, 0:n], IXC, IXC[:, :, 0:n], ALU.mult)
        for ft in range(FT):
            if prev_carry is None:
                init = 0.0
                er = ()
            elif len(prev_carry) == 1:
                pb = prev_carry[0]
                init = pb[:, ft:ft + 1]
                er = (pb,)
            else:
                pb, col = prev_carry
                init = pb[:, ft, col:col + 1]
                er = (pb,)
            if reverse:
                o_ap, a_ap, b_ap = HS[:, ft, 0:n][:, ::-1], A[:, ft, 0:n][:, ::-1], IXC[:, ft, 0:n][:, ::-1]
            else:
                o_ap, a_ap, b_ap = HS[:, ft, 0:n], A[:, ft, 0:n], IXC[:, ft, 0:n]
            self.op(self.DVE, lambda o_ap=o_ap, a_ap=a_ap, b_ap=b_ap, init=init: self.nc.vector.tensor_tensor_scan(
                out=o_ap, data0=a_ap, data1=b_ap, initial=init, op0=ALU.mult, op1=ALU.add),
                reads=(A, IXC) + er, writes=(HS,))

    def load_gate_w(self, st, l, d, tag):
        WR = self.sb(st, tag + "_WR", [128, 8, 128], BF16)
        WI = self.sb(st, tag + "_WI", [128, 8, 128], BF16)
        self.dma(self.SP, WR, WR[:, :, :], self.b_wrg[l], self.b_wrg[l].t[d].rearrange("n h k -> h n k"))
        self.dma(self.SP, WI, WI[:, :, :], self.b_wig[l], self.b_wig[l].t[d].rearrange("n h k -> h n k"))
        return WR, WI

    def pass2(self, l):
        nch = self.nch
        S = self.seq
        b0 = l * PL
        with ExitStack() as st:
            WR, WI = self.load_gate_w(st, l, 0, "p2")
            DGH = self.sb(st, "p2_DGH", [128, 32, 128], BF16)
            DGL = self.sb(st, "p2_DGL", [128, 32, 128], BF16)
            with ExitStack() as t2:
                DGF = self.sb(t2, "p2_DGF", [128, 32, 128], F32)
                DGF2 = self.sb(t2, "p2_DGF2", [128, 32, 128], F32)
                cw = b0 + C_CW
                self.op(self.DVE, lambda: self.nc.vector.tensor_tensor(
                    out=DGF[:, :, :], in0=self.IDF[:, :].unsqueeze(1).to_broadcast([128, 32, 128]),
                    in1=self.SPK[:, cw:cw + 32].unsqueeze(2).to_broadcast([128, 32, 128]), op=ALU.mult),
                    reads=(self.IDF, self.SPK), writes=(DGF,))
                self.copy(self.DVE, DGH, DGH[:, :, :], DGF, DGF[:, :, :])
                self.copy(self.DVE, DGF2, DGF2[:, :, :], DGH, DGH[:, :, :])
                self.tt(self.DVE, DGF, DGF[:, :, :], DGF, DGF[:, :, :], DGF2, DGF2[:, :, :], ALU.subtract)
                self.copy(self.DVE, DGL, DGL[:, :, :], DGF, DGF[:, :, :])
                self.barrier()
            XRW = [self.sb(st, "p2_XRW%d" % i, [128, FT, CH + 3], BF16) for i in range(2)]
            XCb = [self.sb(st, "p2_XCb%d" % i, [128, FT, CH], BF16) for i in range(2)]
            TRr = [self.sb(st, "p2_TR%d" % i, [128, CH], F32) for i in range(2)]
            TIW = self.sb(st, "p2_TIW", [128, FT, CH], F32)
            A = [self.sb(st, "p2_A%d" % i, [128, FT, CH], F32) for i in range(2)]
            A2 = [self.sb(st, "p2_A2%d" % i, [128, FT, CH], F32) for i in range(2)]
            IXC = [self.sb(st, "p2_IXC%d" % i, [128, FT, CH], F32) for i in range(2)]
            HS1 = self.sb(st, "p2_HS", [128, FT, CH], F32)
            CAR = self.sb(st, "p2_CAR", [128, FT], F32)
            HSb = self.sb(st, "p2_HSb", [128, FT, CH], BF16)

            def load(c):
                B = XRW[c % 2]
                lo = c * CH - 2
                hi = c * CH + CH + 1
                clo = 0
                if lo < 0:
                    self.memset(self.DVE, B, B[:, :, 0:2], 0.0)
                    clo = -lo
                    lo = 0
                chi = CH + 3
                if hi > S:
                    self.memset(self.DVE, B, B[:, :, CH + 2:CH + 3], 0.0)
                    chi -= hi - S
                    hi = S
                self.dma(self.SP, B, B[:, :, clo:chi], self.d_xr, self.dv(self.d_xr, lo, hi - lo), accumulate=True)

            def conv(c):
                B = XRW[c % 2]
                for ft in range(FT):
                    ps = self.next_ps()
                    k = 0
                    for j in range(4):
                        for DG in (DGH, DGL):
                            self.mm(ps, ps[:, :], DG, DG[:, j * 8 + ft, :], B, B[:, ft, j:j + CH], k == 0, k == 7)
                            k += 1
                    cb = self.SPK[:, b0 + C_CB + ft:b0 + C_CB + ft + 1]
                    self.ts(self.DVE, XCb[c % 2], XCb[c % 2][:, ft, :], ps, ps[:, :], cb, ALU.add, extra_reads=(self.SPK,))
                self.dma(self.SP, self.d_xc, self.dv(self.d_xc, c * CH, CH), XCb[c % 2], XCb[c % 2][:, :, :])

            d0 = l * 64

            def stageA(c):
                p = c % 2
                X_ = XCb[p]
                psis = []
                for ft in range(FT):
                    psi = self.next_ps()
                    self.mm(psi, psi[:, :], WI, WI[:, ft, :], X_, X_[:, ft, :], True, True)
                    ci = d0 + 16 + ft
                    self.act(TIW, TIW[:, ft, :], psi, psi[:, :], AF.Tanh, bias=self.DER[:, ci:ci + 1], scale=0.5,
                             extra_reads=(self.DER,))
                for ft in range(FT):
                    psr = self.next_ps()
                    self.mm(psr, psr[:, :], WR, WR[:, ft, :], X_, X_[:, ft, :], True, True)
                    TR = TRr[ft % 2]
                    cr = d0 + ft
                    ck = d0 + 32 + ft
                    chh = d0 + 48 + ft
                    self.act(TR, TR[:, :], psr, psr[:, :], AF.Tanh, bias=self.DER[:, cr:cr + 1], scale=0.5,
                             extra_reads=(self.DER,))
                    self.act(A[p], A[p][:, ft, :], TR, TR[:, :], AF.Exp, bias=self.DER[:, chh:chh + 1],
                             scale=self.DER[:, chh:chh + 1], extra_reads=(self.DER,))
                    self.act(A2[p], A2[p][:, ft, :], TR, TR[:, :], AF.Exp, bias=self.DER[:, ck:ck + 1],
                             scale=self.DER[:, ck:ck + 1], extra_reads=(self.DER,))
                self.act(A2[p], A2[p][:, :, :], A2[p], A2[p][:, :, :], AF.Sqrt, bias=self.QTR[:, 0:1], scale=-0.25,
                         extra_reads=(self.QTR,))

            def stageIXC(c):
                p = c % 2
                self.stt(IXC[p], IXC[p][:, :, :], TIW, TIW[:, :, :], 1.0, XCb[p], XCb[p][:, :, :], ALU.add, ALU.mult)

            def stageB(c):
                p = c % 2
                I_ = IXC[p]
                self.tt(self.DVE, I_, I_[:, :, :], A2[p], A2[p][:, :, :], I_, I_[:, :, :], ALU.mult)
                for ft in range(FT):
                    if c == 0:
                        init = 0.0
                        er = ()
                    else:
                        init = CAR[:, ft:ft + 1]
                        er = (CAR,)
                    o_ap = HS1[:, ft, :]
                    a_ap = A[p][:, ft, :]
                    b_ap = I_[:, ft, :]
                    self.op(self.DVE, lambda o_ap=o_ap, a_ap=a_ap, b_ap=b_ap, init=init: self.nc.vector.tensor_tensor_scan(
                        out=o_ap, data0=a_ap, data1=b_ap, initial=init, op0=ALU.mult, op1=ALU.add),
                        reads=(A[p], I_) + er, writes=(HS1,))
                self.copy(self.DVE, CAR, CAR[:, :], HS1, HS1[:, :, CH - 1])
                self.copy(self.DVE, HSb, HSb[:, :, :], HS1, HS1[:, :, :])
                self.dma(self.SP, self.d_hf, self.dv(self.d_hf, c * CH, CH), HSb, HSb[:, :, :])

            load(0)
            if nch > 1:
                load(1)
            conv(0)
            if nch > 1:
                conv(1)
            if nch > 2:
                load(2)
            stageA(0)
            stageIXC(0)
            for c in range(nch):
                if c + 1 < nch:
                    stageA(c + 1)
                    stageIXC(c + 1)
                stageB(c)
                if c + 2 < nch:
                    conv(c + 2)
                if c + 3 < nch:
                    load(c + 3)

    def pass3a(self, l):
        nch = self.nch
        with ExitStack() as st:
            WR, WI = self.load_gate_w(st, l, 1, "p3a")
            WOL = self.sb(st, "p3a_WOL", [128, FT, D], BF16)
            self.dma(self.SP, WOL, WOL[:, :, :], self.b_wol[l], self.b_wol[l].t.rearrange("(kt p) n -> p kt n", p=128))
            XCb = [self.sb(st, "p3a_XCb%d" % i, [128, FT, CH], BF16) for i in range(2)]
            HFb = [self.sb(st, "p3a_HFb", [128, FT, CH], BF16)] * 2
            GR = [self.sb(st, "p3a_GR", [128, FT, CH], BF16)] * 2
            TB = [self.sb(st, "p3a_TB", [128, FT, CH], BF16)] * 2
            TRr = [self.sb(st, "p3a_TR%d" % i, [128, CH], F32) for i in range(2)]
            TIW = self.sb(st, "p3a_TIW", [128, FT, CH], F32)
            A = [self.sb(st, "p3a_A%d" % i, [128, FT, CH], F32) for i in range(2)]
            A2 = [self.sb(st, "p3a_A2%d" % i, [128, FT, CH], F32) for i in range(2)]
            IXC1 = self.sb(st, "p3a_IXC", [128, FT, CH], F32)
            HS1 = self.sb(st, "p3a_HS", [128, FT, CH], F32)
            CAR = self.sb(st, "p3a_CAR", [128, FT], F32)
            YB = self.sb(st, "p3a_YB", [128, FT, CH], BF16)
            MB = self.sb(st, "p3a_MB", [128, FT, CH], BF16)

            def load(c):
                p = c % 2
                c0 = c * CH
                self.dma(self.SP, XCb[p], XCb[p][:, :, :], self.d_xc, self.dv(self.d_xc, c0, CH))

            def load_hf(c):
                self.dma(self.SP, HFb[0], HFb[0][:, :, :], self.d_hf, self.dv(self.d_hf, c * CH, CH))

            def load_gr(c):
                self.dma(self.SP, GR[0], GR[0][:, :, :], self.d_gr, self.dv(self.d_gr, c * CH, CH))

            def load_tb(c):
                self.dma(self.SP, TB[0], TB[0][:, :, :], self.d_tb, self.dv(self.d_tb, c * CH, CH))

            d0 = l * 64
            dd = 1

            def stageA(c):
                p = c % 2
                X_ = XCb[p]
                for ft in range(FT):
                    psr = self.next_ps()
                    self.mm(psr, psr[:, :], WR, WR[:, ft, :], X_, X_[:, ft, :], True, True)
                    psi = self.next_ps()
                    self.mm(psi, psi[:, :], WI, WI[:, ft, :], X_, X_[:, ft, :], True, True)
                    TR = TRr[ft % 2]
                    cr = d0 + dd * 8 + ft
                    ci = d0 + 16 + dd * 8 + ft
                    ck = d0 + 32 + dd * 8 + ft
                    chh = d0 + 48 + dd * 8 + ft
                    self.act(TR, TR[:, :], psr, psr[:, :], AF.Tanh, bias=self.DER[:, cr:cr + 1], scale=0.5,
                             extra_reads=(self.DER,))
                    self.act(TIW, TIW[:, ft, :], psi, psi[:, :], AF.Tanh, bias=self.DER[:, ci:ci + 1], scale=0.5,
                             extra_reads=(self.DER,))
                    self.act(A[p], A[p][:, ft, :], TR, TR[:, :], AF.Exp, bias=self.DER[:, chh:chh + 1],
                             scale=self.DER[:, chh:chh + 1], extra_reads=(self.DER,))
                    self.act(A2[p], A2[p][:, ft, :], TR, TR[:, :], AF.Exp, bias=self.DER[:, ck:ck + 1],
                             scale=self.DER[:, ck:ck + 1], extra_reads=(self.DER,))
                self.act(A2[p], A2[p][:, :, :], A2[p], A2[p][:, :, :], AF.Sqrt, bias=self.QTR[:, 0:1], scale=-0.25,
                         extra_reads=(self.QTR,))

            def stageIXC(c):
                p = c % 2
                self.stt(IXC1, IXC1[:, :, :], TIW, TIW[:, :, :], 1.0, XCb[p], XCb[p][:, :, :], ALU.add, ALU.mult)

            def stageB(i, c):
                p = c % 2
                self.tt(self.DVE, IXC1, IXC1[:, :, :], A2[p], A2[p][:, :, :], IXC1, IXC1[:, :, :], ALU.mult)
                for ft in range(FT):
                    if i == 0:
                        init = 0.0
                        er = ()
                    else:
                        init = CAR[:, ft:ft + 1]
                        er = (CAR,)
                    o_ap = HS1[:, ft, :][:, ::-1]
                    a_ap = A[p][:, ft, :][:, ::-1]
                    b_ap = IXC1[:, ft, :][:, ::-1]
                    self.op(self.DVE, lambda o_ap=o_ap, a_ap=a_ap, b_ap=b_ap, init=init: self.nc.vector.tensor_tensor_scan(
                        out=o_ap, data0=a_ap, data1=b_ap, initial=init, op0=ALU.mult, op1=ALU.add),
                        reads=(A[p], IXC1) + er, writes=(HS1,))
                self.copy(self.DVE, CAR, CAR[:, :], HS1, HS1[:, :, 0])
                if self.debug:
                    self.dma(self.SP, self.d_hb, self.dv(self.d_hb, c * CH, CH), HS1, HS1[:, :, :])
                self.tt(self.DVE, IXC1, IXC1[:, :, :], HS1, HS1[:, :, :], HFb[0], HFb[0][:, :, :], ALU.add)
                if i + 1 < len(order):
                    load_hf(order[i + 1])
                self.tt(self.DVE, YB, YB[:, :, :], IXC1, IXC1[:, :, :], GR[0], GR[0][:, :, :], ALU.mult)
                if i + 1 < len(order):
                    load_gr(order[i + 1])
                if self.debug:
                    self.dma(self.SP, self.d_yb, self.dv(self.d_yb, c * CH, CH), YB, YB[:, :, :])
                for j in range(FT):
                    ps = self.next_ps()
                    for kt in range(FT):
                        self.mm(ps, ps[:, :], WOL, WOL[:, kt, j * 128:(j + 1) * 128], YB, YB[:, kt, :], kt == 0, kt == FT - 1)
                    self.stt(MB, MB[:, j, :], TB[0], TB[0][:, j, :], 1.0, ps, ps[:, :], ALU.add, ALU.mult)
                if i + 1 < len(order):
                    load_tb(order[i + 1])
                self.dma(self.SP, self.d_mb, self.dv(self.d_mb, c * CH, CH), MB, MB[:, :, :])

            order = list(range(nch - 1, -1, -1))
            load(order[0])
            load_hf(order[0])
            load_gr(order[0])
            load_tb(order[0])
            if len(order) > 1:
                load(order[1])
            stageA(order[0])
            stageIXC(order[0])
            for i, c in enumerate(order):
                if i + 1 < len(order):
                    stageA(order[i + 1])
                stageB(i, c)
                if i + 1 < len(order):
                    stageIXC(order[i + 1])
                if i + 2 < len(order):
                    load(order[i + 2])

    def pass3b(self, l, x_src):
        nch = self.nch
        S = self.seq
        nblk = S // 128
        with ExitStack() as st:
            WOA = self.sb(st, "p3b_WOA", [128, FT, D], BF16)
            WOUT = self.sb(st, "p3b_WOUT", [128, FT, D], BF16)
            self.dma(self.SP, WOA, WOA[:, :, :], self.b_woa[l], self.b_woa[l].t.rearrange("(kt p) n -> p kt n", p=128))
            self.dma(self.SP, WOUT, WOUT[:, :, :], self.b_wout[l], self.b_wout[l].t.rearrange("(kt p) n -> p kt n", p=128))
            self.BTH = self.sb(st, "p3b_BTH", [128, 3, 8, 128], BF16)
            self.BTL = self.sb(st, "p3b_BTL", [128, 3, 8, 128], BF16)
            self.dma(self.SP, self.BTH, self.BTH[:, :, :, :], self.d_bth, self.d_bth.t[:, :, :, :])
            self.dma(self.SP, self.BTL, self.BTL[:, :, :, :], self.d_btl, self.d_btl.t[:, :, :, :])
            ESK = self.sb(st, "p3b_ES", [128, 1024], F32)
            self.dma(self.SP, ESK, ESK[:, :], self.sinkrep, self.sinkrep.t[:, l, :])
            self.act(ESK, ESK[:, :], ESK, ESK[:, :], AF.Exp)
            QT = [self.sb(st, "p3b_QT%d" % i, [128, FT, CH], BF16) for i in range(2)]
            KW = [self.sb(st, "p3b_KW%d" % i, [128, 2, CH + 256], BF16) for i in range(2)]
            VW = [self.sb(st, "p3b_VW%d" % i, [128, 6, 256], BF16) for i in range(2)]
            TA = [self.sb(st, "p3b_TA%d" % i, [128, FT, CH], BF16) for i in range(2)]
            MBv = [self.sb(st, "p3b_MB%d" % i, [128, FT, CH], BF16) for i in range(2)]
            X = [self.sb(st, "p3b_X%d" % i, [128, FT, CH], F32) for i in range(2)]
            PT = [self.sb(st, "p3b_PT%d" % i, [128, 512], BF16) for i in range(6)]
            self.pti = 0
            DEN = [self.sb(st, "p3b_DEN%d" % i, [128, 512], F32) for i in range(2)]
            AO = self.sb(st, "p3b_AO", [128, FT, CH], BF16)
            M1 = [self.sb(st, "p3b_M1%d" % i, [128, 512], F32) for i in range(2)]
            MIX = self.sb(st, "p3b_MIX", [128, FT, CH], BF16)

            def load(c):
                p = c % 2
                c0 = c * CH
                self.dma(self.SP, QT[p], QT[p][:, :, :], self.d_q, self.dv(self.d_q, c0, CH))
                lo = max(0, c0 - 128)
                hi = min(S, c0 + CH + 128)
                off = lo - (c0 - 128)
                self.dma(self.SP, KW[p], KW[p][:, :, off:off + hi - lo], self.d_k,
                         self.d_k.t[:, lo:hi].rearrange("(g p) t -> p g t", p=128))
                self.dma(self.SP, VW[p], VW[p][:, off // 128:(off + hi - lo) // 128, :], self.d_v,
                         self.d_v.t[lo:hi, :].rearrange("(s p) d -> p s d", p=128))
                self.dma(self.SP, TA[p], TA[p][:, :, :], self.d_ta, self.dv(self.d_ta, c0, CH))
                self.dma(self.SP, MBv[p], MBv[p][:, :, :], self.d_mb, self.dv(self.d_mb, c0, CH))
                self.dma(self.SP, X[p], X[p][:, :, :], x_src, self.dv(x_src, c0, CH))

            load(0)
            for c in range(nch):
                if c + 1 < nch:
                    load(c + 1)
                p = c % 2
                groups = [(qb, g) for qb in range(4) for g in range(2)]

                def S1(n):
                    qb, g = groups[n]
                    ib = c * 4 + qb
                    jbs = [jb for jb in (ib - 1, ib, ib + 1) if 0 <= jb < nblk]
                    pts = []
                    for jb in jbs:
                        w = jb - (c * 4 - 1)
                        rel = jb - ib + 1
                        pss = self.next_ps()
                        pv = pss[:, :].rearrange("p (h q) -> p h q", h=4)
                        self.mm(pss, pv, KW[p], KW[p][:, g, w * 128:(w + 1) * 128],
                                QT[p], QT[p][:, 4 * g:4 * g + 4, qb * 128:(qb + 1) * 128], True, False)
                        self.mm(pss, pv, self.IDENT, self.IDENT[:, :], self.BTH, self.BTH[:, rel, 4 * g:4 * g + 4, :], False, False)
                        self.mm(pss, pv, self.IDENT, self.IDENT[:, :], self.BTL, self.BTL[:, rel, 4 * g:4 * g + 4, :], False, True)
                        P_ = PT[self.pti % len(PT)]
                        self.pti += 1
                        self.act(P_, P_[:, :], pss, pss[:, :], AF.Exp)
                        pts.append((P_, w))
                    return pts

                def S2(n, pts):
                    qb, g = groups[n]
                    pso = self.next_ps()
                    psd = self.next_ps()
                    for k, (P_, w) in enumerate(pts):
                        self.mm(pso, pso[:, :], VW[p], VW[p][:, w, g * 128:(g + 1) * 128], P_, P_[:, :],
                                k == 0, k == len(pts) - 1)
                    for k, (P_, w) in enumerate(pts):
                        self.mm(psd, psd[:, :], self.ONES, self.ONES[:, :], P_, P_[:, :], k == 0, k == len(pts) - 1)
                    Dn = DEN[n % 2]
                    self.tt(self.DVE, Dn, Dn[:, :], psd, psd[:, :], ESK, ESK[:, g * 512:(g + 1) * 512], ALU.add)
                    self.act(Dn, Dn[:, :], Dn, Dn[:, :], AF.Ln)
                    self.act(Dn, Dn[:, :], Dn, Dn[:, :], AF.Exp, scale=-1.0)
                    self.tt(self.DVE, AO, AO[:, 4 * g:4 * g + 4, qb * 128:(qb + 1) * 128],
                            pso, pso[:, :].rearrange("p (h q) -> p h q", h=4),
                            Dn, Dn[:, :].rearrange("p (h q) -> p h q", h=4), ALU.mult)

                pend = S1(0)
                for n in range(len(groups)):
                    nxt = S1(n + 1) if n + 1 < len(groups) else None
                    S2(n, pend)
                    pend = nxt
                if self.debug:
                    self.dma(self.SP, self.d_ao, self.dv(self.d_ao, c * CH, CH), AO, AO[:, :, :])
                for j in range(FT):
                    ps = self.next_ps()
                    for kt in range(FT):
                        self.mm(ps, ps[:, :], WOA, WOA[:, kt, j * 128:(j + 1) * 128], AO, AO[:, kt, :], kt == 0, kt == FT - 1)
                    m1 = M1[j % 2]
                    self.stt(m1, m1[:, :], TA[p], TA[p][:, j, :], 1.0, ps, ps[:, :], ALU.add, ALU.mult)
                    self.stt(MIX, MIX[:, j, :], MBv[p], MBv[p][:, j, :], 0.5, m1, m1[:, :], ALU.mult, ALU.add)
                for j in range(FT):
                    ps = self.next_ps()
                    for kt in range(FT):
                        self.mm(ps, ps[:, :], WOUT, WOUT[:, kt, j * 128:(j + 1) * 128], MIX, MIX[:, kt, :], kt == 0, kt == FT - 1)
                    self.stt(X[p], X[p][:, j, :], ps, ps[:, :], 0.5, X[p], X[p][:, j, :], ALU.mult, ALU.add)
                self.dma(self.SP, self.d_x1, self.dv(self.d_x1, c * CH, CH), X[p], X[p][:, :, :])

    def pass4(self, l, dst, last):
        nch = self.nch
        b0 = l * PL
        with ExitStack() as st:
            WUP = self.sb(st, "p4_WUP", [128, FT, DFF], BF16)
            wv = self.b_wup[l].t.rearrange("(kt p) n -> p kt n", p=128)
            for kt in range(FT):
                self.dma(self.SP, WUP, WUP[:, kt, :], self.b_wup[l], wv[:, kt, :], accumulate=True)
            WDN = [self.sb(st, "p4_WDN%d" % i, [128, FT, D], BF16) for i in range(2)]
            self.EPSB = self.sb(st, "p4_eps", [128, 1], F32)
            self.memset(self.DVE, self.EPSB, self.EPSB[:, :], EPS)
            X = [self.sb(st, "p4_X%d" % i, [128, FT, CH], F32) for i in range(2)]
            SQ = self.sb(st, "p4_SQ", [128, FT, CH], BF16)
            LNT = self.sb(st, "p4_LNT", [128, CH], F32)
            RS = self.sb(st, "p4_RS", [128, CH], F32)
            H = [self.sb(st, "p4_H", [128, FT, CH], BF16)] * 2
            HID = self.sb(st, "p4_HID", [128, 32, CH], BF16)
            RL = [self.sb(st, "p4_RL%d" % i, [128, CH], F32) for i in range(2)]

            def load(c):
                self.dma(self.SP, X[c % 2], X[c % 2][:, :, :], self.d_x1, self.dv(self.d_x1, c * CH, CH))

            wdi = 0
            load(0)
            if nch > 1:
                load(1)
            self.norm_stage(X[0], SQ, RS, LNT, H[0], b0 + C_N2, CH)
            for c in range(nch):
                Hc = H[c % 2]
                Xc = X[c % 2]
                for j in range(32):
                    if j == 10 and c + 1 < nch:
                        self.norm_sq(X[(c + 1) % 2], SQ, CH)
                    if j == 22 and c + 1 < nch:
                        self.norm_rs(SQ, RS, LNT, CH)
                    ps = self.next_ps()
                    for kt in range(FT):
                        self.mm(ps, ps[:, :], WUP, WUP[:, kt, j * 128:(j + 1) * 128], Hc, Hc[:, kt, :], kt == 0, kt == FT - 1)
                    rl = RL[j % 2]
                    self.act(rl, rl[:, :], ps, ps[:, :], AF.Relu)
                    self.tt(self.DVE, HID, HID[:, j, :], rl, rl[:, :], rl, rl[:, :], ALU.mult)
                if c + 1 < nch:
                    self.norm_apply(X[(c + 1) % 2], RS, H[(c + 1) % 2], b0 + C_N2, CH)
                banks = [self.next_ps() for _ in range(8)]
                for s in range(4):
                    Wd = WDN[wdi % 2]
                    wdi += 1
                    self.dma(self.SP, Wd, Wd[:, :, :], self.b_wdn[l],
                             self.b_wdn[l].t[s * 1024:(s + 1) * 1024, :].rearrange("(kt p) n -> p kt n", p=128))
                    for j in range(FT):
                        for kt in range(FT):
                            self.mm(banks[j], banks[j][:, :], Wd, Wd[:, kt, j * 128:(j + 1) * 128], HID, HID[:, s * 8 + kt, :],
                                    s == 0 and kt == 0, s == 3 and kt == FT - 1, signal=(kt == FT - 1))
                for j in range(FT):
                    self.tt(self.DVE, Xc, Xc[:, j, :], banks[j], banks[j][:, :], Xc, Xc[:, j, :], ALU.add)
                if not last:
                    self.dma(self.SP, dst, self.dv(dst, c * CH, CH), Xc, Xc[:, :, :])
                else:
                    gcol = self.depth * PL
                    self.act(SQ, SQ[:, :, :], Xc, Xc[:, :, :], AF.Square)
                    ps = self.next_ps()
                    for ft in range(FT):
                        self.mm(ps, ps[:, :], self.ONESM, self.ONESM[:, :], SQ, SQ[:, ft, :], ft == 0, ft == FT - 1)
                    self.act(LNT, LNT[:, :], ps, ps[:, :], AF.Ln, bias=self.EPSB[:, 0:1], extra_reads=(self.EPSB,))
                    self.act(RS, RS[:, :], LNT, LNT[:, :], AF.Exp, scale=-0.5)
                    for ft in range(FT):
                        self.stt(Xc, Xc[:, ft, :], Xc, Xc[:, ft, :], self.SPK[:, gcol + ft:gcol + ft + 1], RS, RS[:, :],
                                 ALU.mult, ALU.mult, extra_reads=(self.SPK,))
                    self.dma(self.SP, dst, self.dv(dst, c * CH, CH), Xc, Xc[:, :, :])
                if c + 2 < nch:
                    load(c + 2)


def _t5_bucket_np(rel):
    half = 16
    max_exact = 8
    n = np.abs(rel)
    large = max_exact + (np.log(np.maximum(n, 1).astype(np.float32) / np.float32(max_exact))
                         / np.float32(math.log(128 / max_exact)) * np.float32(half - max_exact)).astype(np.int32)
    large = np.minimum(large, half - 1)
    return np.where(rel > 0, half, 0) + np.where(n < max_exact, n, large)


def host_prep(inputs, depth=DEPTH):
    L = depth
    f32 = np.float32
    spk = np.zeros((128, L * PL + 8), f32)

    def pm(v):
        return np.ascontiguousarray(np.asarray(v, f32).reshape(8, 128).T)

    for l in range(L):
        b0 = l * PL
        spk[:, b0 + C_N1:b0 + C_N1 + 8] = pm(inputs["norm1_g"][l])
        spk[:, b0 + C_N2:b0 + C_N2 + 8] = pm(inputs["norm2_g"][l])
        for j in range(4):
            spk[:, b0 + C_CW + j * 8:b0 + C_CW + j * 8 + 8] = pm(inputs["conv_w"][l, j])
        spk[:, b0 + C_CB:b0 + C_CB + 8] = pm(inputs["conv_b"][l])
        for d in range(2):
            spk[:, b0 + C_BR + d * 8:b0 + C_BR + d * 8 + 8] = pm(inputs["b_rgate"][l, d])
            spk[:, b0 + C_BI + d * 8:b0 + C_BI + d * 8 + 8] = pm(inputs["b_igate"][l, d])
            spk[:, b0 + C_LAM + d * 8:b0 + C_LAM + d * 8 + 8] = pm(inputs["lru_lambda"][l, d])
    spk[:, L * PL:L * PL + 8] = pm(inputs["final_norm_g"])
    sink = np.asarray(inputs["attn_sink"], f32)[:L]
    sinkrep = np.ascontiguousarray(np.broadcast_to(np.repeat(sink, 128, axis=1)[None], (128, L, 1024))).astype(f32)
    j = np.arange(128)[:, None, None]
    r = np.arange(3)[None, :, None]
    q = np.arange(128)[None, None, :]
    rel = ((r - 1) * 128 + j - q).astype(np.int32)
    bkt = _t5_bucket_np(rel)
    rb = np.asarray(inputs["rel_bias"], f32)
    biasg = np.ascontiguousarray(np.transpose(rb[bkt], (0, 1, 3, 2))).astype(f32)
    maskc = np.where(np.abs(rel) <= 128, 0.0, MASK_NEG).astype(f32)
    maskc = np.ascontiguousarray(np.broadcast_to(maskc[:, :, None, :], (128, 3, 8, 128))).astype(f32)
    shared = {
        "w_in": np.ascontiguousarray(inputs["w_in"][:L], f32),
        "w_o_attn": np.ascontiguousarray(inputs["w_o_attn"][:L], f32),
        "w_o_lru": np.ascontiguousarray(inputs["w_o_lru"][:L], f32),
        "w_out": np.ascontiguousarray(inputs["w_out"][:L], f32),
        "w_mlp_up": np.ascontiguousarray(inputs["w_mlp_up"][:L], f32),
        "w_mlp_down": np.ascontiguousarray(inputs["w_mlp_down"][:L], f32),
        "w_rgate": np.ascontiguousarray(inputs["w_rgate"][:L], f32),
        "w_igate": np.ascontiguousarray(inputs["w_igate"][:L], f32),
        "spk": spk, "sinkrep": sinkrep, "biasg": biasg, "maskc": maskc,
        "identc": np.eye(128, dtype=f32),
    }
    return shared


def kernel(**inputs):
    x = np.asarray(inputs["x"], np.float32)
    B, S, _ = x.shape
    shared = host_prep(inputs, DEPTH)
    k = K(seq=S, depth=DEPTH)
    nc = k.build()
    in_maps = []
    for b in range(B):
        m = dict(shared)
        m["xT"] = np.ascontiguousarray(x[b].T)
        in_maps.append(m)
    res = run_bass_kernel_spmd(nc, in_maps, core_ids=list(range(B)))
    out = np.empty((B, S, D), np.float32)
    for b in range(B):
        out[b] = np.asarray(res.results[b]["outT"]).T
    return out
```

```python
import math
from contextlib import ExitStack

import numpy as np
import concourse.bass as bass
import concourse.mybir as mybir
from concourse.bass_utils import run_bass_kernel_spmd

F32 = mybir.dt.float32
BF16 = mybir.dt.bfloat16
AF = mybir.ActivationFunctionType
ALU = mybir.AluOpType

D = 1024
FT = 8
DEPTH = 4
SEQ = 8192
NCORES = 8
CH = 512
IN_COLS = 5632
DFF = 4096
EPS = 1e-6
MASK_NEG = -30000.0
OQ, OK_, OV, OXR, OGR, OGA, OGB = 0, 1024, 1280, 1536, 2560, 3584, 4608
PL = 104
C_N1, C_N2, C_CW, C_CB, C_BR, C_BI, C_LAM = 0, 8, 16, 48, 56, 72, 88
GELU_C = math.sqrt(2.0 / math.pi)


class Sem:
    def __init__(self, h, name):
        self.h = h
        self.name = name
        self.v = 0


class Buf:
    def __init__(self, name, t, dram=False):
        self.name = name
        self.t = t
        self.dram = dram
        self.writers = {}
        self.readers = {}
        self.psum = False

    def __getitem__(self, k):
        return self.t[k]


class Eng:
    def __init__(self, name, e, sem):
        self.name = name
        self.e = e
        self.sem = sem
        self.cnt = 0
        self.seen = {}

    def wait_tok(self, sem, val):
        if sem is self.sem and val > self.cnt:
            return
        if self.seen.get(sem, 0) >= val:
            return
        self.e.wait_ge(sem.h, val)
        self.seen[sem] = val


class K:
    def __init__(self, seq=SEQ, depth=DEPTH, debug=False):
        self.seq = seq
        self.depth = depth
        self.nch = seq // CH
        self.debug = debug
        self.nc = bass.Bass("TRN2", target_bir_lowering=False)
        self.es = ExitStack()
        self.sems = []
        self.dma_sems = {}
        self.n_ins = 0
        self.n_alloc = 0

    def new_sem(self, name):
        h = self.es.enter_context(self.nc.semaphore(name))
        s = Sem(h, name)
        self.sems.append(s)
        return s

    def dram(self, name, shape, dt, kind="Internal"):
        t = self.nc.dram_tensor(name, list(shape), dt, kind=kind)
        return Buf(name, t.ap(), dram=True)

    def sb(self, stack, name, shape, dt):
        self.n_alloc += 1
        t = stack.enter_context(self.nc.sbuf_tensor("%s_%d" % (name, self.n_alloc), list(shape), dt))
        return Buf(name, t)

    def op(self, eng, emit, reads=(), writes=(), signal=True):
        for b in reads:
            for s, v in b.writers.items():
                eng.wait_tok(s, v)
            if b.psum:
                for s, v in b.readers.items():
                    eng.wait_tok(s, v)
        for b in writes:
            for s, v in b.writers.items():
                eng.wait_tok(s, v)
            for s, v in b.readers.items():
                eng.wait_tok(s, v)
        ins = emit()
        self.n_ins += 1
        val = eng.cnt + 1
        if signal:
            ins.then_inc(eng.sem.h, 1)
            eng.cnt = val
        for b in reads:
            if b.readers.get(eng.sem, 0) < val:
                b.readers[eng.sem] = val
        for b in writes:
            b.writers = {eng.sem: val}
            b.readers = {}
        return ins

    def dma(self, q, dst, dst_ap, src, src_ap, accumulate=False, **kw):
        key = dst.name if not dst.dram else (src.name if not src.dram else dst.name)
        sem = self.dma_sems.get(key)
        if sem is None:
            sem = self.new_sem("d%d" % len(self.dma_sems))
            self.dma_sems[key] = sem
        for s, v in src.writers.items():
            q.wait_tok(s, v)
        if not (dst.dram or accumulate):
            for s, v in dst.writers.items():
                q.wait_tok(s, v)
        for s, v in dst.readers.items():
            q.wait_tok(s, v)
        ins = q.e.dma_start(out=dst_ap, in_=src_ap, **kw)
        self.n_ins += 1
        sem.v += 16
        ins.then_inc(sem.h, 16)
        src.readers[sem] = sem.v
        if dst.dram or accumulate:
            dst.writers[sem] = sem.v
            if not dst.dram:
                dst.readers = {}
        else:
            dst.writers = {sem: sem.v}
            dst.readers = {}

    def barrier(self, final=False):
        engs = [self.PE, self.ACT, self.DVE, self.POOL, self.SP]
        skip = set() if final else {sm for k_, sm in self.dma_sems.items() if k_.startswith("b_w")}
        for e in engs:
            for s in self.sems:
                if s in skip:
                    continue
                cur = s.v
                for e2 in engs:
                    if e2.sem is s:
                        cur = e2.cnt
                if cur > 0:
                    e.wait_tok(s, cur)

    def mm(self, ps_buf, out_ap, lhsT_buf, lhsT_ap, rhs_buf, rhs_ap, start, stop, signal=None):
        if signal is None:
            signal = stop
        return self.op(self.PE, lambda: self.nc.tensor.matmul(out_ap, lhsT=lhsT_ap, rhs=rhs_ap, start=start, stop=stop),
                       reads=(lhsT_buf, rhs_buf), writes=(ps_buf,), signal=signal)

    def act(self, out_buf, out_ap, in_buf, in_ap, func, bias=None, scale=None, extra_reads=()):
        kw = {}
        if bias is not None:
            kw["bias"] = bias
        if scale is not None:
            kw["scale"] = scale
        return self.op(self.ACT, lambda: self.nc.scalar.activation(out=out_ap, in_=in_ap, func=func, **kw),
                       reads=(in_buf,) + tuple(extra_reads), writes=(out_buf,))

    def tt(self, eng, out_buf, out_ap, a_buf, a_ap, b_buf, b_ap, op):
        e = eng.e
        return self.op(eng, lambda: e.tensor_tensor(out=out_ap, in0=a_ap, in1=b_ap, op=op),
                       reads=(a_buf, b_buf), writes=(out_buf,))

    def stt(self, out_buf, out_ap, a_buf, a_ap, scalar, b_buf, b_ap, op0, op1, extra_reads=()):
        return self.op(self.DVE, lambda: self.nc.vector.scalar_tensor_tensor(
            out=out_ap, in0=a_ap, scalar=scalar, in1=b_ap, op0=op0, op1=op1),
            reads=(a_buf, b_buf) + tuple(extra_reads), writes=(out_buf,))

    def ts(self, eng, out_buf, out_ap, a_buf, a_ap, s1, op0, s2=None, op1=None, extra_reads=()):
        e = eng.e
        if op1 is None:
            return self.op(eng, lambda: e.tensor_scalar(out=out_ap, in0=a_ap, scalar1=s1, scalar2=None, op0=op0),
                           reads=(a_buf,) + tuple(extra_reads), writes=(out_buf,))
        return self.op(eng, lambda: e.tensor_scalar(out=out_ap, in0=a_ap, scalar1=s1, scalar2=s2, op0=op0, op1=op1),
                       reads=(a_buf,) + tuple(extra_reads), writes=(out_buf,))

    def copy(self, eng, out_buf, out_ap, in_buf, in_ap):
        e = eng.e
        return self.op(eng, lambda: e.tensor_copy(out=out_ap, in_=in_ap), reads=(in_buf,), writes=(out_buf,))

    def memset(self, eng, buf, ap, val):
        e = eng.e
        return self.op(eng, lambda: e.memset(ap, val), reads=(), writes=(buf,))

    def next_ps(self):
        b = self.psum[self.ps_i % len(self.psum)]
        self.ps_i += 1
        return b

    def build(self):
        nc = self.nc
        S = self.seq
        L = self.depth
        es = self.es
        self.PE = Eng("pe", nc.tensor, self.new_sem("s_pe"))
        self.ACT = Eng("act", nc.scalar, self.new_sem("s_act"))
        self.DVE = Eng("dve", nc.vector, self.new_sem("s_dve"))
        self.POOL = Eng("pool", nc.gpsimd, self.new_sem("s_pool"))
        self.SP = Eng("sp", nc.sync, self.new_sem("s_sp"))

        ext = "ExternalInput"
        self.xT = self.dram("xT", [D, S], F32, ext)
        self.w_in = self.dram("w_in", [L, D, IN_COLS], F32, ext)
        self.w_oa = self.dram("w_o_attn", [L, D, D], F32, ext)
        self.w_ol = self.dram("w_o_lru", [L, D, D], F32, ext)
        self.w_out = self.dram("w_out", [L, D, D], F32, ext)
        self.w_up = self.dram("w_mlp_up", [L, D, DFF], F32, ext)
        self.w_dn = self.dram("w_mlp_down", [L, DFF, D], F32, ext)
        self.w_rg = self.dram("w_rgate", [L, 2, 8, 128, 128], F32, ext)
        self.w_ig = self.dram("w_igate", [L, 2, 8, 128, 128], F32, ext)
        self.spk = self.dram("spk", [128, L * PL + 8], F32, ext)
        self.sinkrep = self.dram("sinkrep", [128, L, 1024], F32, ext)
        self.biasg = self.dram("biasg", [128, 3, 8, 128], F32, ext)
        self.maskc = self.dram("maskc", [128, 3, 8, 128], F32, ext)
        self.identc = self.dram("identc", [128, 128], F32, ext)
        self.outT = self.dram("outT", [D, S], F32, "ExternalOutput")
        dk = "ExternalOutput" if self.debug else "Internal"
        self.b_win = [self.dram("b_win%d" % l, [D, IN_COLS], BF16) for l in range(L)]
        self.b_woa = [self.dram("b_woa%d" % l, [D, D], BF16) for l in range(L)]
        self.b_wol = [self.dram("b_wol%d" % l, [D, D], BF16) for l in range(L)]
        self.b_wout = [self.dram("b_wout%d" % l, [D, D], BF16) for l in range(L)]
        self.b_wup = [self.dram("b_wup%d" % l, [D, DFF], BF16) for l in range(L)]
        self.b_wdn = [self.dram("b_wdn%d" % l, [DFF, D], BF16) for l in range(L)]
        self.b_wrg = [self.dram("b_wrg%d" % l, [2, 8, 128, 128], BF16) for l in range(L)]
        self.b_wig = [self.dram("b_wig%d" % l, [2, 8, 128, 128], BF16) for l in range(L)]
        self.d_q = self.dram("d_q", [D, S], BF16, dk)
        self.d_k = self.dram("d_k", [256, S], BF16, dk)
        self.d_v = self.dram("d_v", [S, 256], BF16, dk)
        self.d_xr = self.dram("d_xr", [D, S], BF16, dk)
        self.d_gr = self.dram("d_gr", [D, S], BF16, dk)
        self.d_ta = self.dram("d_ta", [D, S], BF16, dk)
        self.d_tb = self.dram("d_tb", [D, S], BF16, dk)
        self.d_xc = self.dram("d_xc", [D, S], BF16, dk)
        self.d_hf = self.dram("d_hf", [D, S], BF16, dk)
        self.d_mb = self.dram("d_mb", [D, S], BF16, dk)
        self.d_x1 = self.dram("d_x1", [D, S], F32, dk)
        self.d_x2 = self.dram("d_x2", [D, S], F32, dk)
        self.d_bth = self.dram("d_bth", [128, 3, 8, 128], BF16)
        self.d_btl = self.dram("d_btl", [128, 3, 8, 128], BF16)
        if self.debug:
            self.d_hb = self.dram("d_hb", [D, S], F32, dk)
            self.d_yb = self.dram("d_yb", [D, S], BF16, dk)
            self.d_ao = self.dram("d_ao", [D, S], BF16, dk)

        self.psum = []
        for i in range(8):
            t = es.enter_context(nc.psum_tensor("ps%d" % i, [128, 512], F32))
            self.psum.append(Buf("ps%d" % i, t))
            self.psum[-1].psum = True
        self.ps_i = 0

        g = es
        self.SPK = self.sb(g, "SPK", [128, L * PL + 8], F32)
        self.DER = self.sb(g, "DER", [128, L * 64], F32)
        self.ONES = self.sb(g, "ONES", [128, 128], BF16)
        self.ONESM = self.sb(g, "ONESM", [128, 128], BF16)
        self.IDENT = self.sb(g, "IDENT", [128, 128], BF16)
        self.QTR = self.sb(g, "QTR", [128, 1], F32)
        self.IDF = self.sb(g, "IDF", [128, 128], F32)

        self.setup()
        for l in range(L):
            self.layer(l)
        self.barrier(final=True)
        es.close()
        return nc

    def cast_w(self, dst, src_ap, rows, cols):
        r = rows // 128
        s_ap = src_ap.rearrange("(p r) n -> p (r n)", p=128)
        d_ap = dst.t.rearrange("(p r) n -> p (r n)", p=128)
        tot = r * cols
        step = 8192
        srcbuf = self._cast_src
        for c0 in range(0, tot, step):
            c1 = min(tot, c0 + step)
            self.dma(self.POOL, dst, d_ap[:, c0:c1], srcbuf, s_ap[:, c0:c1], max_dma_last_dim=4096)

    def setup(self):
        nc = self.nc
        L = self.depth
        for l in range(L):
            for (dst, src, rows, cols) in ((() if l == 0 else ((self.b_win[l], self.w_in, D, IN_COLS),)) + (
                (self.b_wol[l], self.w_ol, D, D),
                (self.b_woa[l], self.w_oa, D, D),
                (self.b_wout[l], self.w_out, D, D),
                (self.b_wup[l], self.w_up, D, DFF),
                (self.b_wdn[l], self.w_dn, DFF, D),
            )):
                self._cast_src = src
                self.cast_w(dst, src.t[l], rows, cols)
            for (dst, src) in ((self.b_wrg[l], self.w_rg), (self.b_wig[l], self.w_ig)):
                s_ap = src.t[l].rearrange("d n h k -> (d n) (h k)")
                d_ap = dst.t.rearrange("d n h k -> (d n) (h k)")
                self.dma(self.POOL, dst, d_ap, src, s_ap, max_dma_last_dim=4096)
        self.dma(self.SP, self.SPK, self.SPK[:, :], self.spk, self.spk.t[:, :])
        with ExitStack() as st:
            TMP = self.sb(st, "su_tmp", [128, 3, 8, 128], F32)
            TMP2 = self.sb(st, "su_tmp2", [128, 3, 8, 128], F32)
            TMP3 = self.sb(st, "su_tmp3", [128, 3, 8, 128], F32)
            IDF = self.IDF
            self.BTH = self.sb(st, "su_BTH", [128, 3, 8, 128], BF16)
            self.BTL = self.sb(st, "su_BTL", [128, 3, 8, 128], BF16)
            E1 = self.sb(st, "su_e1", [128, L * 16], F32)
            self.memset(self.DVE, self.ONES, self.ONES[:, :], 1.0)
            self.memset(self.DVE, self.ONESM, self.ONESM[:, :], 1.0 / 1024.0)
            self.memset(self.DVE, self.QTR, self.QTR[:, :], 0.25000003)
            self.dma(self.SP, IDF, IDF[:, :], self.identc, self.identc.t[:, :])
            self.copy(self.DVE, self.IDENT, self.IDENT[:, :], IDF, IDF[:, :])
            self.dma(self.SP, TMP, TMP[:, :, :, :], self.biasg, self.biasg.t[:, :, :, :])
            self.dma(self.SP, TMP2, TMP2[:, :, :, :], self.maskc, self.maskc.t[:, :, :, :])
            self.tt(self.DVE, TMP, TMP[:, :, :, :], TMP, TMP[:, :, :, :], TMP2, TMP2[:, :, :, :], ALU.add)
            self.copy(self.DVE, self.BTH, self.BTH[:, :, :, :], TMP, TMP[:, :, :, :])
            self.copy(self.DVE, TMP2, TMP2[:, :, :, :], self.BTH, self.BTH[:, :, :, :])
            self.tt(self.DVE, TMP3, TMP3[:, :, :, :], TMP, TMP[:, :, :, :], TMP2, TMP2[:, :, :, :], ALU.subtract)
            self.copy(self.DVE, self.BTL, self.BTL[:, :, :, :], TMP3, TMP3[:, :, :, :])
            self.dma(self.SP, self.d_bth, self.d_bth.t[:, :, :, :], self.BTH, self.BTH[:, :, :, :])
            self.dma(self.SP, self.d_btl, self.d_btl.t[:, :, :, :], self.BTL, self.BTL[:, :, :, :])
            for l in range(L):
                b0 = l * PL
                d0 = l * 64
                self.ts(self.DVE, self.DER, self.DER[:, d0:d0 + 32], self.SPK, self.SPK[:, b0 + C_BR:b0 + C_BR + 32],
                        0.5, ALU.mult)
                self.act(E1, E1[:, l * 16:(l + 1) * 16], self.SPK, self.SPK[:, b0 + C_LAM:b0 + C_LAM + 16],
                         AF.Exp, scale=-1.0)
            for l in range(L):
                d0 = l * 64
                e_ap = E1[:, l * 16:(l + 1) * 16]
                q = self.DER[:, d0 + 32:d0 + 48]
                self.ts(self.DVE, self.DER, q, E1, e_ap, -1.0 / 6.0, ALU.mult)
                for cst in (1.0 / 5.0, -1.0 / 4.0, 1.0 / 3.0, -1.0 / 2.0, 1.0):
                    self.stt(self.DER, q, self.DER, q, cst, E1, e_ap, ALU.add, ALU.mult)
                self.ts(self.DVE, self.DER, self.DER[:, d0 + 48:d0 + 64], self.DER, q, -4.0, ALU.mult)
                self.ts(self.DVE, self.DER, q, self.DER, q, -8.0, ALU.mult)
        self.barrier()

    def dv(self, buf, c0, n):
        return buf.t[:, c0:c0 + n].rearrange("(ft p) t -> p ft t", p=128)

    def layer(self, l):
        x_src = self.xT if l == 0 else self.d_x2
        sa = getattr(self, "stop_after", 9)
        self.pass1(l, x_src)
        self.barrier()
        if sa <= 1:
            return
        self.pass2(l)
        self.barrier()
        if sa <= 2:
            return
        self.pass3a(l)
        self.barrier()
        if sa <= 3:
            return
        self.pass3b(l, x_src)
        self.barrier()
        if sa <= 4:
            return
        last = (l == self.depth - 1)
        self.pass4(l, self.outT if last else self.d_x2, last)
        self.barrier()

    def norm_stage(self, X, SQ, RS, LNT, H, gcol, n):
        self.norm_stats(X, SQ, RS, LNT, n)
        self.norm_apply(X, RS, H, gcol, n)

    def norm_apply(self, X, RS, H, gcol, n):
        for ft in range(FT):
            self.stt(H, H[:, ft, 0:n], X, X[:, ft, 0:n], self.SPK[:, gcol + ft:gcol + ft + 1], RS, RS[:, 0:n],
                     ALU.mult, ALU.mult, extra_reads=(self.SPK,))

    def norm_stats(self, X, SQ, RS, LNT, n):
        self.norm_sq(X, SQ, n)
        self.norm_rs(SQ, RS, LNT, n)

    def norm_sq(self, X, SQ, n):
        self.act(SQ, SQ[:, :, 0:n], X, X[:, :, 0:n], AF.Square)

    def norm_rs(self, SQ, RS, LNT, n):
        ps = self.next_ps()
        for ft in range(FT):
            self.mm(ps, ps[:, 0:n], self.ONESM, self.ONESM[:, :], SQ, SQ[:, ft, 0:n], ft == 0, ft == FT - 1)
        self.act(LNT, LNT[:, 0:n], ps, ps[:, 0:n], AF.Ln, bias=self.EPSB[:, 0:1], extra_reads=(self.EPSB,))
        self.act(RS, RS[:, 0:n], LNT, LNT[:, 0:n], AF.Exp, scale=-0.5)

    def pass1(self, l, x_src):
        nch = self.nch
        b0 = l * PL
        with ExitStack() as st:
            W = self.sb(st, "p1_W", [128, FT, IN_COLS], BF16)
            if l == 0:
                with ExitStack() as t1:
                    STG = [self.sb(t1, "p1_STG%d" % i, [128, IN_COLS], F32) for i in range(2)]
                    for kt in range(FT):
                        sg = STG[kt % 2]
                        self.dma(self.SP, sg, sg[:, :], self.w_in, self.w_in.t[0, kt * 128:(kt + 1) * 128, :])
                        half = IN_COLS // 2
                        self.act(W, W[:, kt, 0:half], sg, sg[:, 0:half], AF.Copy)
                        self.copy(self.DVE, W, W[:, kt, half:IN_COLS], sg, sg[:, half:IN_COLS])
                    self.barrier()
            else:
                wsrc = self.b_win[l]
                wv = wsrc.t.rearrange("(kt p) n -> p kt n", p=128)
                for kt in range(FT):
                    self.dma(self.SP, W, W[:, kt, :], wsrc, wv[:, kt, :], accumulate=True)
            self.EPSB = self.sb(st, "p1_eps", [128, 1], F32)
            self.memset(self.DVE, self.EPSB, self.EPSB[:, :], EPS)
            X = [self.sb(st, "p1_X%d" % i, [128, FT, CH], F32) for i in range(2)]
            SQ = self.sb(st, "p1_SQ", [128, FT, CH], BF16)
            LNT = self.sb(st, "p1_LNT", [128, CH], F32)
            RS = self.sb(st, "p1_RS", [128, CH], F32)
            H = [self.sb(st, "p1_H%d" % i, [128, FT, CH], BF16) for i in range(2)]
            Q = self.sb(st, "p1_Q", [128, FT, CH], BF16)
            KK = self.sb(st, "p1_K", [128, 2, CH], BF16)
            V = self.sb(st, "p1_V", [128, 4, 256], BF16)
            XR = self.sb(st, "p1_XR", [128, FT, CH], BF16)
            GR = self.sb(st, "p1_GR", [128, FT, CH], BF16)
            TA = self.sb(st, "p1_TA", [128, FT, CH], BF16)
            TB = self.sb(st, "p1_TB", [128, FT, CH], BF16)
            GT = [self.sb(st, "p1_GT%d" % i, [128, CH], F32) for i in range(2)]

            def load(c):
                self.dma(self.SP, X[c % 2], X[c % 2][:, :, :], x_src, self.dv(x_src, c * CH, CH))

            load(0)
            if nch > 1:
                load(1)
            self.norm_stage(X[0], SQ, RS, LNT, H[0], b0 + C_N1, CH)
            rot = 0
            for c in range(nch):
                gi = 0
                Hc = H[c % 2]
                c0 = c * CH
                groups = [(Q, OQ, 8, "q"), (KK, OK_, 2, "k"), (XR, OXR, 8, "c"), (GR, OGR, 8, "g"),
                          (TA, OGA, 8, "t"), (TB, OGB, 8, "t")]
                for (dst, off, nt, kind) in groups:
                    for j in range(nt):
                        gi += 1
                        if gi == 5 and c + 1 < nch:
                            self.norm_sq(X[(c + 1) % 2], SQ, CH)
                        if gi == 20 and c + 1 < nch:
                            self.norm_rs(SQ, RS, LNT, CH)
                            self.norm_apply(X[(c + 1) % 2], RS, H[(c + 1) % 2], b0 + C_N1, CH)
                            if c + 2 < nch:
                                load(c + 2)
                        ps = self.next_ps()
                        for kt in range(FT):
                            self.mm(ps, ps[:, :], W, W[:, kt, off + j * 128:off + (j + 1) * 128], Hc, Hc[:, kt, :],
                                    kt == 0, kt == FT - 1)
                        if kind == "t":
                            self.act(dst, dst[:, j, :], ps, ps[:, :], AF.Tanh, scale=0.5)
                        elif kind == "q":
                            self.ts(self.DVE, dst, dst[:, j, :], ps, ps[:, :], 1.0 / math.sqrt(128.0), ALU.mult)
                        elif kind == "g":
                            g1 = GT[j % 2]
                            self.act(g1, g1[:, :], ps, ps[:, :], AF.Square, scale=math.sqrt(0.044715))
                            self.stt(g1, g1[:, :], g1, g1[:, :], 1.0, ps, ps[:, :], ALU.add, ALU.mult)
                            self.act(g1, g1[:, :], g1, g1[:, :], AF.Tanh, scale=GELU_C)
                            self.stt(dst, dst[:, j, :], g1, g1[:, :], 1.0, ps, ps[:, :], ALU.add, ALU.mult)
                        else:
                            rot += 1
                            if rot % 2 == 0:
                                self.act(dst, dst[:, j, :], ps, ps[:, :], AF.Copy)
                            else:
                                self.copy(self.DVE, dst, dst[:, j, :], ps, ps[:, :])
                for s in range(4):
                    ps = self.next_ps()
                    for kt in range(FT):
                        self.mm(ps, ps[:, 0:256], Hc, Hc[:, kt, s * 128:(s + 1) * 128], W, W[:, kt, OV:OV + 256],
                                kt == 0, kt == FT - 1)
                    self.copy(self.DVE, V, V[:, s, :], ps, ps[:, 0:256])
                for (dst, dd) in ((Q, self.d_q), (XR, self.d_xr), (GR, self.d_gr), (TA, self.d_ta), (TB, self.d_tb)):
                    self.dma(self.SP, dd, self.dv(dd, c0, CH), dst, dst[:, :, :])
                self.dma(self.SP, self.d_k, self.d_k.t[:, c0:c0 + CH].rearrange("(g p) t -> p g t", p=128), KK, KK[:, :, :])
                self.dma(self.SP, self.d_v, self.d_v.t[c0:c0 + CH, :].rearrange("(s p) d -> p s d", p=128), V, V[:, :, :])

    def lru_dir(self, l, d, XCb, XCm, WR, WI, TRr, TIr, A, A2, IXC, HS, prev_carry, n, reverse):
        d0 = l * 64
        for ft in range(FT):
            psr = self.next_ps()
            self.mm(psr, psr[:, 0:n], WR, WR[:, ft, :], XCb, XCb[:, ft, 0:n], True, True)
            psi = self.next_ps()
            self.mm(psi, psi[:, 0:n], WI, WI[:, ft, :], XCb, XCb[:, ft, 0:n], True, True)
            TR = TRr[ft % 2]
            TI = TIr[ft % 2]
            cr = d0 + d * 8 + ft
            ci = d0 + 16 + d * 8 + ft
            ck = d0 + 32 + d * 8 + ft
            ch = d0 + 48 + d * 8 + ft
            self.act(TR, TR[:, 0:n], psr, psr[:, 0:n], AF.Tanh, bias=self.DER[:, cr:cr + 1], scale=0.5,
                     extra_reads=(self.DER,))
            self.act(TI, TI[:, 0:n], psi, psi[:, 0:n], AF.Tanh, bias=self.DER[:, ci:ci + 1], scale=0.5,
                     extra_reads=(self.DER,))
            self.act(A, A[:, ft, 0:n], TR, TR[:, 0:n], AF.Exp, bias=self.DER[:, ch:ch + 1], scale=self.DER[:, ch:ch + 1],
                     extra_reads=(self.DER,))
            self.act(A2, A2[:, ft, 0:n], TR, TR[:, 0:n], AF.Exp, bias=self.DER[:, ck:ck + 1], scale=self.DER[:, ck:ck + 1],
                     extra_reads=(self.DER,))
            self.stt(IXC, IXC[:, ft, 0:n], TI, TI[:, 0:n], 1.0, XCm, XCm[:, ft, 0:n], ALU.add, ALU.mult)
        self.act(A2, A2[:, :, 0:n], A2, A2[:, :, 0:n], AF.Sqrt, bias=self.QTR[:, 0:1], scale=-0.25, extra_reads=(self.QTR,))
        self.tt(self.DVE, IXC, IXC[:, :, 0:n], A2, A2[:, :, 0:n], IXC, IXC[:, :, 0:n], ALU.mult)
        for ft in range(FT):
            if prev_carry is None:
                init = 0.0
                er = ()
            elif len(prev_carry) == 1:
                pb = prev_carry[0]
                init = pb[:, ft:ft + 1]
                er = (pb,)
            else:
                pb, col = prev_carry
                init = pb[:, ft, col:col + 1]
                er = (pb,)
            if reverse:
                o_ap, a_ap, b_ap = HS[:, ft, 0:n][:, ::-1], A[:, ft, 0:n][:, ::-1], IXC[:, ft, 0:n][:, ::-1]
            else:
                o_ap, a_ap, b_ap = HS[:, ft, 0:n], A[:, ft, 0:n], IXC[:, ft, 0:n]
            self.op(self.DVE, lambda o_ap=o_ap, a_ap=a_ap, b_ap=b_ap, init=init: self.nc.vector.tensor_tensor_scan(
                out=o_ap, data0=a_ap, data1=b_ap, initial=init, op0=ALU.mult, op1=ALU.add),
                reads=(A, IXC) + er, writes=(HS,))

    def load_gate_w(self, st, l, d, tag):
        WR = self.sb(st, tag + "_WR", [128, 8, 128], BF16)
        WI = self.sb(st, tag + "_WI", [128, 8, 128], BF16)
        self.dma(self.SP, WR, WR[:, :, :], self.b_wrg[l], self.b_wrg[l].t[d].rearrange("n h k -> h n k"))
        self.dma(self.SP, WI, WI[:, :, :], self.b_wig[l], self.b_wig[l].t[d].rearrange("n h k -> h n k"))
        return WR, WI

    def pass2(self, l):
        nch = self.nch
        S = self.seq
        b0 = l * PL
        with ExitStack() as st:
            WR, WI = self.load_gate_w(st, l, 0, "p2")
            DGH = self.sb(st, "p2_DGH", [128, 32, 128], BF16)
            DGL = self.sb(st, "p2_DGL", [128, 32, 128], BF16)
            with ExitStack() as t2:
                DGF = self.sb(t2, "p2_DGF", [128, 32, 128], F32)
                DGF2 = self.sb(t2, "p2_DGF2", [128, 32, 128], F32)
                cw = b0 + C_CW
                self.op(self.DVE, lambda: self.nc.vector.tensor_tensor(
                    out=DGF[:, :, :], in0=self.IDF[:, :].unsqueeze(1).to_broadcast([128, 32, 128]),
                    in1=self.SPK[:, cw:cw + 32].unsqueeze(2).to_broadcast([128, 32, 128]), op=ALU.mult),
                    reads=(self.IDF, self.SPK), writes=(DGF,))
                self.copy(self.DVE, DGH, DGH[:, :, :], DGF, DGF[:, :, :])
                self.copy(self.DVE, DGF2, DGF2[:, :, :], DGH, DGH[:, :, :])
                self.tt(self.DVE, DGF, DGF[:, :, :], DGF, DGF[:, :, :], DGF2, DGF2[:, :, :], ALU.subtract)
                self.copy(self.DVE, DGL, DGL[:, :, :], DGF, DGF[:, :, :])
                self.barrier()
            XRW = [self.sb(st, "p2_XRW%d" % i, [128, FT, CH + 3], BF16) for i in range(2)]
            XCb = [self.sb(st, "p2_XCb%d" % i, [128, FT, CH], BF16) for i in range(3)]
            TRr = [self.sb(st, "p2_TR%d" % i, [128, CH], F32) for i in range(2)]
            TIW = self.sb(st, "p2_TIW", [128, FT, CH], F32)
            A = [self.sb(st, "p2_A%d" % i, [128, FT, CH], F32) for i in range(2)]
            A2 = [self.sb(st, "p2_A2%d" % i, [128, FT, CH], F32) for i in range(2)]
            IXC = [self.sb(st, "p2_IXC%d" % i, [128, FT, CH], F32) for i in range(2)]
            HS1 = self.sb(st, "p2_HS", [128, FT, CH], F32)
            CAR = self.sb(st, "p2_CAR", [128, FT], F32)
            HSb = self.sb(st, "p2_HSb", [128, FT, CH], BF16)

            def load(c):
                B = XRW[c % 2]
                lo = c * CH - 2
                hi = c * CH + CH + 1
                clo = 0
                if lo < 0:
                    self.memset(self.DVE, B, B[:, :, 0:2], 0.0)
                    clo = -lo
                    lo = 0
                chi = CH + 3
                if hi > S:
                    self.memset(self.DVE, B, B[:, :, CH + 2:CH + 3], 0.0)
                    chi -= hi - S
                    hi = S
                self.dma(self.SP, B, B[:, :, clo:chi], self.d_xr, self.dv(self.d_xr, lo, hi - lo), accumulate=True)

            def conv(c):
                B = XRW[c % 2]
                for ft in range(FT):
                    ps = self.next_ps()
                    k = 0
                    for j in range(4):
                        for DG in (DGH, DGL):
                            self.mm(ps, ps[:, :], DG, DG[:, j * 8 + ft, :], B, B[:, ft, j:j + CH], k == 0, k == 7)
                            k += 1
                    cb = self.SPK[:, b0 + C_CB + ft:b0 + C_CB + ft + 1]
                    self.ts(self.DVE, XCb[c % 3], XCb[c % 3][:, ft, :], ps, ps[:, :], cb, ALU.add, extra_reads=(self.SPK,))
                self.dma(self.SP, self.d_xc, self.dv(self.d_xc, c * CH, CH), XCb[c % 3], XCb[c % 3][:, :, :])

            d0 = l * 64

            def stageA(c):
                p = c % 2
                X_ = XCb[c % 3]
                psis = []
                for ft in range(FT):
                    psi = self.next_ps()
                    self.mm(psi, psi[:, :], WI, WI[:, ft, :], X_, X_[:, ft, :], True, True)
                    ci = d0 + 16 + ft
                    self.act(TIW, TIW[:, ft, :], psi, psi[:, :], AF.Tanh, bias=self.DER[:, ci:ci + 1], scale=0.5,
                             extra_reads=(self.DER,))
                for ft in range(FT):
                    psr = self.next_ps()
                    self.mm(psr, psr[:, :], WR, WR[:, ft, :], X_, X_[:, ft, :], True, True)
                    TR = TRr[ft % 2]
                    cr = d0 + ft
                    ck = d0 + 32 + ft
                    chh = d0 + 48 + ft
                    self.act(TR, TR[:, :], psr, psr[:, :], AF.Tanh, bias=self.DER[:, cr:cr + 1], scale=0.5,
                             extra_reads=(self.DER,))
                    self.act(A[p], A[p][:, ft, :], TR, TR[:, :], AF.Exp, bias=self.DER[:, chh:chh + 1],
                             scale=self.DER[:, chh:chh + 1], extra_reads=(self.DER,))
                    self.act(A2[p], A2[p][:, ft, :], TR, TR[:, :], AF.Exp, bias=self.DER[:, ck:ck + 1],
                             scale=self.DER[:, ck:ck + 1], extra_reads=(self.DER,))
                self.act(A2[p], A2[p][:, :, :], A2[p], A2[p][:, :, :], AF.Sqrt, bias=self.QTR[:, 0:1], scale=-0.25,
                         extra_reads=(self.QTR,))

            def stageIXC(c):
                p = c % 2
                self.stt(IXC[p], IXC[p][:, :, :], TIW, TIW[:, :, :], 1.0, XCb[c % 3], XCb[c % 3][:, :, :], ALU.add, ALU.mult)

            def stageB(c):
                p = c % 2
                I_ = IXC[p]
                self.tt(self.DVE, I_, I_[:, :, :], A2[p], A2[p][:, :, :], I_, I_[:, :, :], ALU.mult)
                for ft in range(FT):
                    if c == 0:
                        init = 0.0
                        er = ()
                    else:
                        init = CAR[:, ft:ft + 1]
                        er = (CAR,)
                    o_ap = HS1[:, ft, :]
                    a_ap = A[p][:, ft, :]
                    b_ap = I_[:, ft, :]
                    self.op(self.DVE, lambda o_ap=o_ap, a_ap=a_ap, b_ap=b_ap, init=init: self.nc.vector.tensor_tensor_scan(
                        out=o_ap, data0=a_ap, data1=b_ap, initial=init, op0=ALU.mult, op1=ALU.add),
                        reads=(A[p], I_) + er, writes=(HS1,))
                self.copy(self.DVE, CAR, CAR[:, :], HS1, HS1[:, :, CH - 1])
                self.copy(self.DVE, HSb, HSb[:, :, :], HS1, HS1[:, :, :])
                self.dma(self.SP, self.d_hf, self.dv(self.d_hf, c * CH, CH), HSb, HSb[:, :, :])

            load(0)
            if nch > 1:
                load(1)
            conv(0)
            if nch > 2:
                load(2)
            if nch > 1:
                conv(1)
            if nch > 3:
                load(3)
            if nch > 2:
                conv(2)
            if nch > 4:
                load(4)
            stageA(0)
            stageIXC(0)
            for c in range(nch):
                if c + 1 < nch:
                    stageA(c + 1)
                    stageIXC(c + 1)
                stageB(c)
                if c + 3 < nch:
                    conv(c + 3)
                if c + 5 < nch:
                    load(c + 5)

    def pass3a(self, l):
        nch = self.nch
        with ExitStack() as st:
            WR, WI = self.load_gate_w(st, l, 1, "p3a")
            WOL = self.sb(st, "p3a_WOL", [128, FT, D], BF16)
            self.dma(self.SP, WOL, WOL[:, :, :], self.b_wol[l], self.b_wol[l].t.rearrange("(kt p) n -> p kt n", p=128))
            XCb = [self.sb(st, "p3a_XCb%d" % i, [128, FT, CH], BF16) for i in range(2)]
            HFb = [self.sb(st, "p3a_HFb", [128, FT, CH], BF16)] * 2
            GR = [self.sb(st, "p3a_GR", [128, FT, CH], BF16)] * 2
            TB = [self.sb(st, "p3a_TB", [128, FT, CH], BF16)] * 2
            TRr = [self.sb(st, "p3a_TR%d" % i, [128, CH], F32) for i in range(2)]
            TIW = self.sb(st, "p3a_TIW", [128, FT, CH], F32)
            A = [self.sb(st, "p3a_A%d" % i, [128, FT, CH], F32) for i in range(2)]
            A2 = [self.sb(st, "p3a_A2%d" % i, [128, FT, CH], F32) for i in range(2)]
            IXC1 = self.sb(st, "p3a_IXC", [128, FT, CH], F32)
            HS1 = self.sb(st, "p3a_HS", [128, FT, CH], F32)
            CAR = self.sb(st, "p3a_CAR", [128, FT], F32)
            YB = self.sb(st, "p3a_YB", [128, FT, CH], BF16)
            MB = self.sb(st, "p3a_MB", [128, FT, CH], BF16)

            def load(c):
                p = c % 2
                c0 = c * CH
                self.dma(self.SP, XCb[p], XCb[p][:, :, :], self.d_xc, self.dv(self.d_xc, c0, CH))

            def load_hf(c):
                self.dma(self.SP, HFb[0], HFb[0][:, :, :], self.d_hf, self.dv(self.d_hf, c * CH, CH))

            def load_gr(c):
                self.dma(self.SP, GR[0], GR[0][:, :, :], self.d_gr, self.dv(self.d_gr, c * CH, CH))

            def load_tb(c):
                self.dma(self.SP, TB[0], TB[0][:, :, :], self.d_tb, self.dv(self.d_tb, c * CH, CH))

            d0 = l * 64
            dd = 1

            def stageA(c):
                p = c % 2
                X_ = XCb[p]
                for ft in range(FT):
                    psr = self.next_ps()
                    self.mm(psr, psr[:, :], WR, WR[:, ft, :], X_, X_[:, ft, :], True, True)
                    psi = self.next_ps()
                    self.mm(psi, psi[:, :], WI, WI[:, ft, :], X_, X_[:, ft, :], True, True)
                    TR = TRr[ft % 2]
                    cr = d0 + dd * 8 + ft
                    ci = d0 + 16 + dd * 8 + ft
                    ck = d0 + 32 + dd * 8 + ft
                    chh = d0 + 48 + dd * 8 + ft
                    self.act(TR, TR[:, :], psr, psr[:, :], AF.Tanh, bias=self.DER[:, cr:cr + 1], scale=0.5,
                             extra_reads=(self.DER,))
                    self.act(TIW, TIW[:, ft, :], psi, psi[:, :], AF.Tanh, bias=self.DER[:, ci:ci + 1], scale=0.5,
                             extra_reads=(self.DER,))
                    self.act(A[p], A[p][:, ft, :], TR, TR[:, :], AF.Exp, bias=self.DER[:, chh:chh + 1],
                             scale=self.DER[:, chh:chh + 1], extra_reads=(self.DER,))
                    self.act(A2[p], A2[p][:, ft, :], TR, TR[:, :], AF.Exp, bias=self.DER[:, ck:ck + 1],
                             scale=self.DER[:, ck:ck + 1], extra_reads=(self.DER,))
                self.act(A2[p], A2[p][:, :, :], A2[p], A2[p][:, :, :], AF.Sqrt, bias=self.QTR[:, 0:1], scale=-0.25,
                         extra_reads=(self.QTR,))

            def stageIXC(c):
                p = c % 2
                self.stt(IXC1, IXC1[:, :, :], TIW, TIW[:, :, :], 1.0, XCb[p], XCb[p][:, :, :], ALU.add, ALU.mult)

            def stageB(i, c):
                p = c % 2
                self.tt(self.DVE, IXC1, IXC1[:, :, :], A2[p], A2[p][:, :, :], IXC1, IXC1[:, :, :], ALU.mult)
                for ft in range(FT):
                    if i == 0:
                        init = 0.0
                        er = ()
                    else:
                        init = CAR[:, ft:ft + 1]
                        er = (CAR,)
                    o_ap = HS1[:, ft, :][:, ::-1]
                    a_ap = A[p][:, ft, :][:, ::-1]
                    b_ap = IXC1[:, ft, :][:, ::-1]
                    self.op(self.DVE, lambda o_ap=o_ap, a_ap=a_ap, b_ap=b_ap, init=init: self.nc.vector.tensor_tensor_scan(
                        out=o_ap, data0=a_ap, data1=b_ap, initial=init, op0=ALU.mult, op1=ALU.add),
                        reads=(A[p], IXC1) + er, writes=(HS1,))
                self.copy(self.DVE, CAR, CAR[:, :], HS1, HS1[:, :, 0])
                if self.debug:
                    self.dma(self.SP, self.d_hb, self.dv(self.d_hb, c * CH, CH), HS1, HS1[:, :, :])
                self.tt(self.DVE, IXC1, IXC1[:, :, :], HS1, HS1[:, :, :], HFb[0], HFb[0][:, :, :], ALU.add)
                if i + 1 < len(order):
                    load_hf(order[i + 1])
                self.tt(self.DVE, YB, YB[:, :, :], IXC1, IXC1[:, :, :], GR[0], GR[0][:, :, :], ALU.mult)
                if i + 1 < len(order):
                    load_gr(order[i + 1])
                if self.debug:
                    self.dma(self.SP, self.d_yb, self.dv(self.d_yb, c * CH, CH), YB, YB[:, :, :])
                if i + 1 < len(order):
                    stageIXC(order[i + 1])
                for j in range(FT):
                    ps = self.next_ps()
                    for kt in range(FT):
                        self.mm(ps, ps[:, :], WOL, WOL[:, kt, j * 128:(j + 1) * 128], YB, YB[:, kt, :], kt == 0, kt == FT - 1)
                    self.stt(MB, MB[:, j, :], TB[0], TB[0][:, j, :], 1.0, ps, ps[:, :], ALU.add, ALU.mult)
                if i + 1 < len(order):
                    load_tb(order[i + 1])
                self.dma(self.SP, self.d_mb, self.dv(self.d_mb, c * CH, CH), MB, MB[:, :, :])

            order = list(range(nch - 1, -1, -1))
            load(order[0])
            load_hf(order[0])
            load_gr(order[0])
            load_tb(order[0])
            if len(order) > 1:
                load(order[1])
            stageA(order[0])
            stageIXC(order[0])
            for i, c in enumerate(order):
                if i + 1 < len(order):
                    stageA(order[i + 1])
                stageB(i, c)
                if i + 2 < len(order):
                    load(order[i + 2])

    def pass3b(self, l, x_src):
        nch = self.nch
        S = self.seq
        nblk = S // 128
        with ExitStack() as st:
            WOA = self.sb(st, "p3b_WOA", [128, FT, D], BF16)
            WOUT = self.sb(st, "p3b_WOUT", [128, FT, D], BF16)
            self.dma(self.SP, WOA, WOA[:, :, :], self.b_woa[l], self.b_woa[l].t.rearrange("(kt p) n -> p kt n", p=128))
            self.dma(self.SP, WOUT, WOUT[:, :, :], self.b_wout[l], self.b_wout[l].t.rearrange("(kt p) n -> p kt n", p=128))
            self.BTH = self.sb(st, "p3b_BTH", [128, 3, 8, 128], BF16)
            self.BTL = self.sb(st, "p3b_BTL", [128, 3, 8, 128], BF16)
            self.dma(self.SP, self.BTH, self.BTH[:, :, :, :], self.d_bth, self.d_bth.t[:, :, :, :])
            self.dma(self.SP, self.BTL, self.BTL[:, :, :, :], self.d_btl, self.d_btl.t[:, :, :, :])
            ESK = self.sb(st, "p3b_ES", [128, 1024], F32)
            self.dma(self.SP, ESK, ESK[:, :], self.sinkrep, self.sinkrep.t[:, l, :])
            self.act(ESK, ESK[:, :], ESK, ESK[:, :], AF.Exp)
            QT = [self.sb(st, "p3b_QT%d" % i, [128, FT, CH], BF16) for i in range(2)]
            KW = [self.sb(st, "p3b_KW%d" % i, [128, 2, CH + 256], BF16) for i in range(2)]
            VW = [self.sb(st, "p3b_VW%d" % i, [128, 6, 256], BF16) for i in range(2)]
            TA = [self.sb(st, "p3b_TA%d" % i, [128, FT, CH], BF16) for i in range(2)]
            MBv = [self.sb(st, "p3b_MB%d" % i, [128, FT, CH], BF16) for i in range(2)]
            X = [self.sb(st, "p3b_X%d" % i, [128, FT, CH], F32) for i in range(2)]
            PT = [self.sb(st, "p3b_PT%d" % i, [128, 512], BF16) for i in range(6)]
            self.pti = 0
            DEN = [self.sb(st, "p3b_DEN%d" % i, [128, 512], F32) for i in range(2)]
            AO = self.sb(st, "p3b_AO", [128, FT, CH], BF16)
            M1 = [self.sb(st, "p3b_M1%d" % i, [128, 512], F32) for i in range(2)]
            MIX = self.sb(st, "p3b_MIX", [128, FT, CH], BF16)

            def load(c):
                p = c % 2
                c0 = c * CH
                self.dma(self.SP, QT[p], QT[p][:, :, :], self.d_q, self.dv(self.d_q, c0, CH))
                lo = max(0, c0 - 128)
                hi = min(S, c0 + CH + 128)
                off = lo - (c0 - 128)
                self.dma(self.SP, KW[p], KW[p][:, :, off:off + hi - lo], self.d_k,
                         self.d_k.t[:, lo:hi].rearrange("(g p) t -> p g t", p=128))
                self.dma(self.SP, VW[p], VW[p][:, off // 128:(off + hi - lo) // 128, :], self.d_v,
                         self.d_v.t[lo:hi, :].rearrange("(s p) d -> p s d", p=128))
                self.dma(self.SP, TA[p], TA[p][:, :, :], self.d_ta, self.dv(self.d_ta, c0, CH))
                self.dma(self.SP, MBv[p], MBv[p][:, :, :], self.d_mb, self.dv(self.d_mb, c0, CH))
                self.dma(self.SP, X[p], X[p][:, :, :], x_src, self.dv(x_src, c0, CH))

            load(0)
            for c in range(nch):
                if c + 1 < nch:
                    load(c + 1)
                p = c % 2
                groups = [(qb, g) for qb in range(4) for g in range(2)]

                def S1(n):
                    qb, g = groups[n]
                    ib = c * 4 + qb
                    jbs = [jb for jb in (ib - 1, ib, ib + 1) if 0 <= jb < nblk]
                    pts = []
                    for jb in jbs:
                        w = jb - (c * 4 - 1)
                        rel = jb - ib + 1
                        pss = self.next_ps()
                        pv = pss[:, :].rearrange("p (h q) -> p h q", h=4)
                        self.mm(pss, pv, KW[p], KW[p][:, g, w * 128:(w + 1) * 128],
                                QT[p], QT[p][:, 4 * g:4 * g + 4, qb * 128:(qb + 1) * 128], True, False)
                        self.mm(pss, pv, self.IDENT, self.IDENT[:, :], self.BTH, self.BTH[:, rel, 4 * g:4 * g + 4, :], False, False)
                        self.mm(pss, pv, self.IDENT, self.IDENT[:, :], self.BTL, self.BTL[:, rel, 4 * g:4 * g + 4, :], False, True)
                        P_ = PT[self.pti % len(PT)]
                        self.pti += 1
                        self.act(P_, P_[:, :], pss, pss[:, :], AF.Exp)
                        pts.append((P_, w))
                    return pts

                def S2(n, pts):
                    qb, g = groups[n]
                    pso = self.next_ps()
                    psd = self.next_ps()
                    for k, (P_, w) in enumerate(pts):
                        self.mm(pso, pso[:, :], VW[p], VW[p][:, w, g * 128:(g + 1) * 128], P_, P_[:, :],
                                k == 0, k == len(pts) - 1)
                    for k, (P_, w) in enumerate(pts):
                        self.mm(psd, psd[:, :], self.ONES, self.ONES[:, :], P_, P_[:, :], k == 0, k == len(pts) - 1)
                    Dn = DEN[n % 2]
                    self.tt(self.DVE, Dn, Dn[:, :], psd, psd[:, :], ESK, ESK[:, g * 512:(g + 1) * 512], ALU.add)
                    self.act(Dn, Dn[:, :], Dn, Dn[:, :], AF.Ln)
                    self.act(Dn, Dn[:, :], Dn, Dn[:, :], AF.Exp, scale=-1.0)
                    self.tt(self.DVE, AO, AO[:, 4 * g:4 * g + 4, qb * 128:(qb + 1) * 128],
                            pso, pso[:, :].rearrange("p (h q) -> p h q", h=4),
                            Dn, Dn[:, :].rearrange("p (h q) -> p h q", h=4), ALU.mult)

                pend = S1(0)
                for n in range(len(groups)):
                    nxt = S1(n + 1) if n + 1 < len(groups) else None
                    S2(n, pend)
                    pend = nxt
                if self.debug:
                    self.dma(self.SP, self.d_ao, self.dv(self.d_ao, c * CH, CH), AO, AO[:, :, :])
                for j in range(FT):
                    ps = self.next_ps()
                    for kt in range(FT):
                        self.mm(ps, ps[:, :], WOA, WOA[:, kt, j * 128:(j + 1) * 128], AO, AO[:, kt, :], kt == 0, kt == FT - 1)
                    m1 = M1[j % 2]
                    self.stt(m1, m1[:, :], TA[p], TA[p][:, j, :], 1.0, ps, ps[:, :], ALU.add, ALU.mult)
                    self.stt(MIX, MIX[:, j, :], MBv[p], MBv[p][:, j, :], 0.5, m1, m1[:, :], ALU.mult, ALU.add)
                for j in range(FT):
                    ps = self.next_ps()
                    for kt in range(FT):
                        self.mm(ps, ps[:, :], WOUT, WOUT[:, kt, j * 128:(j + 1) * 128], MIX, MIX[:, kt, :], kt == 0, kt == FT - 1)
                    self.stt(X[p], X[p][:, j, :], ps, ps[:, :], 0.5, X[p], X[p][:, j, :], ALU.mult, ALU.add)
                self.dma(self.SP, self.d_x1, self.dv(self.d_x1, c * CH, CH), X[p], X[p][:, :, :])

    def pass4(self, l, dst, last):
        nch = self.nch
        b0 = l * PL
        with ExitStack() as st:
            WUP = self.sb(st, "p4_WUP", [128, FT, DFF], BF16)
            wv = self.b_wup[l].t.rearrange("(kt p) n -> p kt n", p=128)
            for kt in range(FT):
                self.dma(self.SP, WUP, WUP[:, kt, :], self.b_wup[l], wv[:, kt, :], accumulate=True)
            WDN = [self.sb(st, "p4_WDN%d" % i, [128, FT, D], BF16) for i in range(2)]
            self.EPSB = self.sb(st, "p4_eps", [128, 1], F32)
            self.memset(self.DVE, self.EPSB, self.EPSB[:, :], EPS)
            X = [self.sb(st, "p4_X%d" % i, [128, FT, CH], F32) for i in range(2)]
            SQ = self.sb(st, "p4_SQ", [128, FT, CH], BF16)
            LNT = self.sb(st, "p4_LNT", [128, CH], F32)
            RS = self.sb(st, "p4_RS", [128, CH], F32)
            H = [self.sb(st, "p4_H", [128, FT, CH], BF16)] * 2
            HID = self.sb(st, "p4_HID", [128, 32, CH], BF16)
            RL = [self.sb(st, "p4_RL%d" % i, [128, CH], F32) for i in range(2)]

            def load(c):
                self.dma(self.SP, X[c % 2], X[c % 2][:, :, :], self.d_x1, self.dv(self.d_x1, c * CH, CH))

            wdi = 0
            load(0)
            if nch > 1:
                load(1)
            self.norm_stage(X[0], SQ, RS, LNT, H[0], b0 + C_N2, CH)
            for c in range(nch):
                Hc = H[c % 2]
                Xc = X[c % 2]
                for j in range(32):
                    if j == 10 and c + 1 < nch:
                        self.norm_sq(X[(c + 1) % 2], SQ, CH)
                    if j == 22 and c + 1 < nch:
                        self.norm_rs(SQ, RS, LNT, CH)
                    ps = self.next_ps()
                    for kt in range(FT):
                        self.mm(ps, ps[:, :], WUP, WUP[:, kt, j * 128:(j + 1) * 128], Hc, Hc[:, kt, :], kt == 0, kt == FT - 1)
                    rl = RL[j % 2]
                    self.act(rl, rl[:, :], ps, ps[:, :], AF.Relu)
                    self.tt(self.DVE, HID, HID[:, j, :], rl, rl[:, :], rl, rl[:, :], ALU.mult)
                if c + 1 < nch:
                    self.norm_apply(X[(c + 1) % 2], RS, H[(c + 1) % 2], b0 + C_N2, CH)
                banks = [self.next_ps() for _ in range(8)]
                for s in range(4):
                    Wd = WDN[wdi % 2]
                    wdi += 1
                    self.dma(self.SP, Wd, Wd[:, :, :], self.b_wdn[l],
                             self.b_wdn[l].t[s * 1024:(s + 1) * 1024, :].rearrange("(kt p) n -> p kt n", p=128))
                    for j in range(FT):
                        for kt in range(FT):
                            self.mm(banks[j], banks[j][:, :], Wd, Wd[:, kt, j * 128:(j + 1) * 128], HID, HID[:, s * 8 + kt, :],
                                    s == 0 and kt == 0, s == 3 and kt == FT - 1, signal=(kt == FT - 1))
                for j in range(FT):
                    self.tt(self.DVE, Xc, Xc[:, j, :], banks[j], banks[j][:, :], Xc, Xc[:, j, :], ALU.add)
                if not last:
                    self.dma(self.SP, dst, self.dv(dst, c * CH, CH), Xc, Xc[:, :, :])
                else:
                    gcol = self.depth * PL
                    self.act(SQ, SQ[:, :, :], Xc, Xc[:, :, :], AF.Square)
                    ps = self.next_ps()
                    for ft in range(FT):
                        self.mm(ps, ps[:, :], self.ONESM, self.ONESM[:, :], SQ, SQ[:, ft, :], ft == 0, ft == FT - 1)
                    self.act(LNT, LNT[:, :], ps, ps[:, :], AF.Ln, bias=self.EPSB[:, 0:1], extra_reads=(self.EPSB,))
                    self.act(RS, RS[:, :], LNT, LNT[:, :], AF.Exp, scale=-0.5)
                    for ft in range(FT):
                        self.stt(Xc, Xc[:, ft, :], Xc, Xc[:, ft, :], self.SPK[:, gcol + ft:gcol + ft + 1], RS, RS[:, :],
                                 ALU.mult, ALU.mult, extra_reads=(self.SPK,))
                    self.dma(self.SP, dst, self.dv(dst, c * CH, CH), Xc, Xc[:, :, :])
                if c + 2 < nch:
                    load(c + 2)


def _t5_bucket_np(rel):
    half = 16
    max_exact = 8
    n = np.abs(rel)
    large = max_exact + (np.log(np.maximum(n, 1).astype(np.float32) / np.float32(max_exact))
                         / np.float32(math.log(128 / max_exact)) * np.float32(half - max_exact)).astype(np.int32)
    large = np.minimum(large, half - 1)
    return np.where(rel > 0, half, 0) + np.where(n < max_exact, n, large)


def host_prep(inputs, depth=DEPTH):
    L = depth
    f32 = np.float32
    spk = np.zeros((128, L * PL + 8), f32)

    def pm(v):
        return np.ascontiguousarray(np.asarray(v, f32).reshape(8, 128).T)

    for l in range(L):
        b0 = l * PL
        spk[:, b0 + C_N1:b0 + C_N1 + 8] = pm(inputs["norm1_g"][l])
        spk[:, b0 + C_N2:b0 + C_N2 + 8] = pm(inputs["norm2_g"][l])
        for j in range(4):
            spk[:, b0 + C_CW + j * 8:b0 + C_CW + j * 8 + 8] = pm(inputs["conv_w"][l, j])
        spk[:, b0 + C_CB:b0 + C_CB + 8] = pm(inputs["conv_b"][l])
        for d in range(2):
            spk[:, b0 + C_BR + d * 8:b0 + C_BR + d * 8 + 8] = pm(inputs["b_rgate"][l, d])
            spk[:, b0 + C_BI + d * 8:b0 + C_BI + d * 8 + 8] = pm(inputs["b_igate"][l, d])
            spk[:, b0 + C_LAM + d * 8:b0 + C_LAM + d * 8 + 8] = pm(inputs["lru_lambda"][l, d])
    spk[:, L * PL:L * PL + 8] = pm(inputs["final_norm_g"])
    sink = np.asarray(inputs["attn_sink"], f32)[:L]
    sinkrep = np.ascontiguousarray(np.broadcast_to(np.repeat(sink, 128, axis=1)[None], (128, L, 1024))).astype(f32)
    j = np.arange(128)[:, None, None]
    r = np.arange(3)[None, :, None]
    q = np.arange(128)[None, None, :]
    rel = ((r - 1) * 128 + j - q).astype(np.int32)
    bkt = _t5_bucket_np(rel)
    rb = np.asarray(inputs["rel_bias"], f32)
    biasg = np.ascontiguousarray(np.transpose(rb[bkt], (0, 1, 3, 2))).astype(f32)
    maskc = np.where(np.abs(rel) <= 128, 0.0, MASK_NEG).astype(f32)
    maskc = np.ascontiguousarray(np.broadcast_to(maskc[:, :, None, :], (128, 3, 8, 128))).astype(f32)
    shared = {
        "w_in": np.ascontiguousarray(inputs["w_in"][:L], f32),
        "w_o_attn": np.ascontiguousarray(inputs["w_o_attn"][:L], f32),
        "w_o_lru": np.ascontiguousarray(inputs["w_o_lru"][:L], f32),
        "w_out": np.ascontiguousarray(inputs["w_out"][:L], f32),
        "w_mlp_up": np.ascontiguousarray(inputs["w_mlp_up"][:L], f32),
        "w_mlp_down": np.ascontiguousarray(inputs["w_mlp_down"][:L], f32),
        "w_rgate": np.ascontiguousarray(inputs["w_rgate"][:L], f32),
        "w_igate": np.ascontiguousarray(inputs["w_igate"][:L], f32),
        "spk": spk, "sinkrep": sinkrep, "biasg": biasg, "maskc": maskc,
        "identc": np.eye(128, dtype=f32),
    }
    return shared


def kernel(**inputs):
    x = np.asarray(inputs["x"], np.float32)
    B, S, _ = x.shape
    shared = host_prep(inputs, DEPTH)
    k = K(seq=S, depth=DEPTH)
    nc = k.build()
    in_maps = []
    for b in range(B):
        m = dict(shared)
        m["xT"] = np.ascontiguousarray(x[b].T)
        in_maps.append(m)
    res = run_bass_kernel_spmd(nc, in_maps, core_ids=list(range(B)))
    out = np.empty((B, S, D), np.float32)
    for b in range(B):
        out[b] = np.asarray(res.results[b]["outT"]).T
    return out
```
